# Optimizing a Trainium2 kernel written in Bass

```python
import math
import jax, jax.numpy as jnp
from jax import lax
import numpy as np

D_MODEL = 1024
BATCH = 8
SEQ = 4096
DEPTH = 1

N_META = 16
N_HEADS = 16
HEAD_DIM = 64
D_ATTN = N_HEADS * HEAD_DIM
D_CONV = D_MODEL
CONV_K = 31
Q_BLOCK = 128
LN_EPS = 1e-5
DEEPNORM_ALPHA = (2.0 * DEPTH) ** 0.25
DEEPNORM_BETA = (8.0 * DEPTH) ** -0.25
SPLITS = (D_ATTN, D_ATTN, D_ATTN,
          D_ATTN,
          D_CONV, D_CONV,
          D_CONV,
          D_MODEL, D_MODEL)
D_IN = sum(SPLITS)

kernel_name = "stickbreak_conformer_gated_hybrid"


def layer_norm(x, g, b):
    xf = x.astype(jnp.float32)
    mu = jnp.mean(xf, axis=-1, keepdims=True)
    var = jnp.mean(jnp.square(xf - mu), axis=-1, keepdims=True)
    return ((xf - mu) * lax.rsqrt(var + LN_EPS) * g.astype(jnp.float32)
            + b.astype(jnp.float32)).astype(x.dtype)


def stick_breaking_block(q_blk, k_pre, v_pre, q_start):
    nq, nk = q_blk.shape[2], k_pre.shape[2]
    z = jnp.einsum('bhqd,bhkd->bhqk', q_blk, k_pre).astype(jnp.float32) * (HEAD_DIM ** -0.5)
    q_pos = q_start + jnp.arange(nq)
    k_pos = jnp.arange(nk)
    mask = k_pos[None, :] < q_pos[:, None]
    log_beta = jax.nn.log_sigmoid(z)
    log_1m_beta = jnp.where(mask, log_beta - z, 0.0)
    csum = jnp.cumsum(log_1m_beta, axis=-1)
    suffix = csum[..., -1:] - csum
    attn = jnp.where(mask, jnp.exp(log_beta + suffix), 0.0)
    out = jnp.einsum('bhqk,bhkd->bhqd', attn, v_pre.astype(jnp.float32))
    return out.astype(q_blk.dtype)


def stick_breaking_attention(q, k, v):
    length = q.shape[2]
    bounds = [0, *range(N_META, length, Q_BLOCK), length]
    outs = []
    for lo, hi in zip(bounds[:-1], bounds[1:]):
        outs.append(stick_breaking_block(q[:, :, lo:hi], k[:, :, :hi], v[:, :, :hi], lo))
    return jnp.concatenate(outs, axis=2)


def causal_depthwise_conv(u, w, b):
    y = lax.conv_general_dilated(
        u, w[:, None, :].astype(u.dtype), window_strides=(1,), padding=[(CONV_K - 1, 0)],
        dimension_numbers=('NWC', 'WIO', 'NWC'), feature_group_count=u.shape[-1])
    return y + b


def hybrid_layer(x, w_in, dw_w, dw_b, conv_ln_g, conv_ln_b, w_attn_out, w_conv_out,
                 w_out, post_ln_g, post_ln_b):
    bsz, length, _ = x.shape
    proj = x @ w_in
    idx = np.cumsum(SPLITS)[:-1].tolist()
    q, k, v, z_a, u_c, g_c, z_c, gate_a, gate_c = jnp.split(proj, idx, axis=-1)

    def heads(t):
        return t.reshape(bsz, length, N_HEADS, HEAD_DIM).transpose(0, 2, 1, 3)
    o = stick_breaking_attention(heads(q), heads(k), heads(v))
    o = o.transpose(0, 2, 1, 3).reshape(bsz, length, D_ATTN)
    y_a = (o * jax.nn.silu(z_a)) @ w_attn_out

    u = u_c * jax.nn.sigmoid(g_c)
    c = causal_depthwise_conv(u, dw_w, dw_b)
    c = jax.nn.silu(layer_norm(c, conv_ln_g, conv_ln_b))
    y_c = (c * jax.nn.silu(z_c)) @ w_conv_out

    h = jax.nn.sigmoid(gate_a) * y_a + jax.nn.sigmoid(gate_c) * y_c
    out = h @ w_out
    return layer_norm(DEEPNORM_ALPHA * x + out, post_ln_g, post_ln_b)


def setup_inputs(seed: int = 0) -> dict:
    key = jax.random.key(seed)
    ks = jax.random.split(key, 16)
    f32 = jnp.float32
    x = jax.random.normal(ks[0], (BATCH, SEQ, D_MODEL), f32)
    meta_tokens = jax.random.normal(ks[1], (N_META, D_MODEL), f32)
    emb_ln_g = 1.0 + 0.02 * jax.random.normal(ks[2], (D_MODEL,), f32)
    emb_ln_b = 0.02 * jax.random.normal(ks[3], (D_MODEL,), f32)
    w_in = jax.random.normal(ks[4], (DEPTH, D_MODEL, D_IN), f32) * D_MODEL ** -0.5
    v_lo = 2 * D_ATTN
    col_scale = jnp.ones((D_IN,), f32).at[v_lo:v_lo + D_ATTN].set(DEEPNORM_BETA)
    w_in = w_in * col_scale
    dw_w = jax.random.normal(ks[5], (DEPTH, CONV_K, D_CONV), f32) * CONV_K ** -0.5
    dw_b = 0.02 * jax.random.normal(ks[6], (DEPTH, D_CONV), f32)
    conv_ln_g = 1.0 + 0.02 * jax.random.normal(ks[7], (DEPTH, D_CONV), f32)
    conv_ln_b = 0.02 * jax.random.normal(ks[8], (DEPTH, D_CONV), f32)
    w_attn_out = jax.random.normal(ks[9], (DEPTH, D_ATTN, D_MODEL), f32) * (D_ATTN ** -0.5 * DEEPNORM_BETA)
    w_conv_out = jax.random.normal(ks[10], (DEPTH, D_CONV, D_MODEL), f32) * (D_CONV ** -0.5 * DEEPNORM_BETA)
    w_out = jax.random.normal(ks[11], (DEPTH, D_MODEL, D_MODEL), f32) * (D_MODEL ** -0.5 * DEEPNORM_BETA)
    post_ln_g = 1.0 + 0.02 * jax.random.normal(ks[12], (DEPTH, D_MODEL), f32)
    post_ln_b = 0.02 * jax.random.normal(ks[13], (DEPTH, D_MODEL), f32)
    return {"x": x, "meta_tokens": meta_tokens, "emb_ln_g": emb_ln_g, "emb_ln_b": emb_ln_b,
            "w_in": w_in, "dw_w": dw_w, "dw_b": dw_b, "conv_ln_g": conv_ln_g,
            "conv_ln_b": conv_ln_b, "w_attn_out": w_attn_out, "w_conv_out": w_conv_out,
            "w_out": w_out, "post_ln_g": post_ln_g, "post_ln_b": post_ln_b}


def reference(x, meta_tokens, emb_ln_g, emb_ln_b, w_in, dw_w, dw_b, conv_ln_g, conv_ln_b,
              w_attn_out, w_conv_out, w_out, post_ln_g, post_ln_b):
    bsz = x.shape[0]
    meta = jnp.broadcast_to(meta_tokens.astype(x.dtype)[None], (bsz, N_META, x.shape[-1]))
    h = jnp.concatenate([meta, x], axis=1)
    h = layer_norm(h, emb_ln_g, emb_ln_b)
    for layer in range(DEPTH):
        h = hybrid_layer(h, w_in[layer], dw_w[layer], dw_b[layer], conv_ln_g[layer],
                         conv_ln_b[layer], w_attn_out[layer], w_conv_out[layer],
                         w_out[layer], post_ln_g[layer], post_ln_b[layer])
    return h[:, N_META:]
```

```python
import contextlib
import numpy as np
import concourse.bass as bass
import concourse.mybir as mybir
from concourse.bass_utils import run_bass_kernel_spmd

F32 = mybir.dt.float32
BF16 = mybir.dt.bfloat16
AF = mybir.ActivationFunctionType
ALU = mybir.AluOpType

D = 1024
SEQ = 4096
NMETA = 16
LTOT = SEQ + NMETA
NKC = 8
EPS = 1e-5
ALPHA = 2.0 ** 0.25
CONVK = 31
NT = SEQ // 128
NG = SEQ // 512


class Prog:
    ENGS = ("pe", "act", "dve", "pool", "sp")
    PSUM_PREFIX = ("pp", "tp", "stp", "z_ps", "incl_ps", "bk")

    def __init__(self, nc, stack):
        self.nc = nc
        self.stack = stack
        self.sem = {}
        for e in self.ENGS:
            self.sem[e] = stack.enter_context(nc.semaphore("s_" + e))
        self.cnt = {e: 0 for e in self.ENGS}
        self.known = {e: {} for e in self.ENGS}
        self.lastw = {}
        self.readers = {}
        self.streams = {e: [] for e in self.ENGS}

    def dma_sem(self, key):
        if key not in self.sem:
            self.sem[key] = self.stack.enter_context(self.nc.semaphore("d_" + key))
            self.cnt[key] = 0
        return key

    def op(self, eng, fn, reads=(), writes=(), dma=None):
        deps = {}

        def need(ev, kind):
            k, v = ev
            if k == eng:
                if eng == "pe" or kind != "raw":
                    return
            if deps.get(k, 0) < v:
                deps[k] = v

        for b in reads:
            ev = self.lastw.get(b)
            if ev is not None:
                need(ev, "raw")
            if b.startswith(self.PSUM_PREFIX):
                for ev in self.readers.get(b, ()):
                    need(ev, "war")
        for b in writes:
            ev = self.lastw.get(b)
            if ev is not None:
                need(ev, "waw")
            for ev in self.readers.get(b, ()):
                need(ev, "war")
        waits = []
        kn = self.known[eng]
        for k, v in deps.items():
            if kn.get(k, 0) < v:
                kn[k] = v
                waits.append((k, v))
        if dma is None:
            self.cnt[eng] += 1
            ev = (eng, self.cnt[eng])
            inc = 1
        else:
            self.dma_sem(dma)
            self.cnt[dma] += 16
            ev = (dma, self.cnt[dma])
            inc = 16
        self.streams[eng].append((waits, fn, ev[0], inc))
        for b in reads:
            self.readers.setdefault(b, []).append(ev)
        for b in writes:
            self.lastw[b] = ev
            self.readers[b] = []
        return ev

    def final_waits(self, eng, keys):
        waits = [(k, self.cnt[k]) for k in keys if self.cnt.get(k, 0) > 0]
        self.streams[eng].append((waits, None, None, 0))

    def emit(self):
        nc = self.nc
        with nc.Block() as block:
            def run(engh, name):
                for waits, fn, semk, inc in self.streams[name]:
                    for k, v in waits:
                        engh.wait_ge(self.sem[k], v)
                    if fn is not None:
                        fn(engh).then_inc(self.sem[semk], inc)

            @block.tensor
            def _(t):
                run(t, "pe")

            @block.scalar
            def _(t):
                run(t, "act")

            @block.vector
            def _(t):
                run(t, "dve")

            @block.gpsimd
            def _(t):
                run(t, "pool")

            @block.sync
            def _(t):
                run(t, "sp")
        self.streams = {e: [] for e in self.ENGS}


def build(stage=99, dbg=False):
    nc = bass.Bass("TRN2", target_bir_lowering=False)
    x = nc.dram_tensor("x", [SEQ, D], F32, kind="ExternalInput").ap()
    meta = nc.dram_tensor("meta", [NMETA, D], F32, kind="ExternalInput").ap()
    pvec = nc.dram_tensor("pvec", [128, 5 * NKC + NKC * CONVK], F32, kind="ExternalInput").ap()
    gbb = nc.dram_tensor("gbb", [4, 128, D], F32, kind="ExternalInput").ap()
    w_in_u = nc.dram_tensor("w_in_u", [72, 128, NKC, 128], F32, kind="ExternalInput").ap()
    w_ao_u = nc.dram_tensor("w_ao_u", [8, 128, NKC, 128], F32, kind="ExternalInput").ap()
    w_co_u = nc.dram_tensor("w_co_u", [8, 128, NKC, 128], F32, kind="ExternalInput").ap()
    w_o_u = nc.dram_tensor("w_o_u", [2, 128, NKC, 512], F32, kind="ExternalInput").ap()
    cst = nc.dram_tensor("cst", [128, 5 * 128], F32, kind="ExternalInput").ap()
    out = nc.dram_tensor("out", [SEQ, D], F32, kind="ExternalOutput").ap()
    dbg_hT = None
    if dbg:
        dbg_hT = nc.dram_tensor("dbg_hT", [128, NKC, LTOT], BF16, kind="ExternalOutput").ap()
        dbg_og = nc.dram_tensor("dbg_og", [128, NKC, SEQ], BF16, kind="ExternalOutput").ap()

    with contextlib.ExitStack() as stack:
        P = Prog(nc, stack)
        sb = lambda name, shape, dt: stack.enter_context(nc.sbuf_tensor(name, shape, dt))
        ps = lambda name, shape, dt: stack.enter_context(nc.psum_tensor(name, shape, dt))

        pv = sb("pv", [128, 5 * NKC + NKC * CONVK], F32)
        cstf = sb("cstf", [128, 5 * 128], F32)
        identf = cstf[:, 0:128]
        epsc = sb("epsc", [128, 1], F32)
        onec = sb("onec", [128, 1], F32)
        zeroc = sb("zeroc", [128, 1], F32)
        scr1 = sb("scr1", [128, 1], F32)
        ogT = sb("ogT", [128, NKC, SEQ], BF16)
        P.op("pool", lambda e: e.memset(epsc[:], EPS), writes=["epsc"])
        P.op("pool", lambda e: e.memset(onec[:], 1.0), writes=["onec"])
        P.op("pool", lambda e: e.memset(zeroc[:], 0.0), writes=["zeroc"])

        P.op("sp", lambda e: e.dma_start(out=pv[:], in_=pvec[:, :]), writes=["pv"], dma="c0")
        P.op("sp", lambda e: e.dma_start(out=cstf[:], in_=cst[:, :]), writes=["cstf"], dma="c1")
        embg = lambda kc: pv[:, kc:kc + 1]
        embb = lambda kc: pv[:, NKC + kc:NKC + kc + 1]

        s12 = contextlib.ExitStack()
        hT = s12.enter_context(nc.sbuf_tensor("hT", [128, NKC, LTOT], BF16))
        with contextlib.ExitStack() as s1:
            sb1 = lambda name, shape, dt: s1.enter_context(nc.sbuf_tensor(name, shape, dt))
            ps1 = lambda name, shape, dt: s1.enter_context(nc.psum_tensor(name, shape, dt))
            xt = [sb1(f"xt{i}", [128, D], F32) for i in range(2)]
            xn = [sb1(f"xn{i}", [128, D], F32) for i in range(2)]
            st = [sb1(f"st{i}", [128, 2, 6], F32) for i in range(2)]
            mv = [sb1(f"mv{i}", [128, 2], F32) for i in range(2)]
            rs = [sb1(f"rs{i}", [128, 1], F32) for i in range(2)]
            sq = [sb1(f"sq{i}", [128, 1], F32) for i in range(2)]
            tp = [ps1(f"tp{i}", [128, NKC, 128], F32) for i in range(2)]

            def p1A(ti):
                b = ti % 2
                rows = NMETA if ti == 0 else 128
                src = meta[:, :] if ti == 0 else x[(ti - 1) * 128:ti * 128, :]
                P.op("sp", lambda e: e.dma_start(out=xt[b][0:rows, :], in_=src), writes=[f"xt{b}"], dma=f"xt{b}")
                for h in range(2):
                    P.op("dve", lambda e, h=h: e.bn_stats(
                        out=st[b][0:rows, h, :], in_=xt[b][0:rows, h * 512:(h + 1) * 512]),
                        reads=[f"xt{b}"], writes=[f"st{b}{h}"])
                P.op("dve", lambda e: e.bn_aggr(out=mv[b][0:rows, :], in_=st[b][0:rows, :, :]),
                     reads=[f"st{b}0", f"st{b}1"], writes=[f"mv{b}"])
                P.op("act", lambda e: e.activation(
                    out=sq[b][0:rows, :], in_=mv[b][0:rows, 1:2], func=AF.Sqrt, bias=epsc[0:rows, :], scale=1.0),
                    reads=[f"mv{b}", "epsc"], writes=[f"sq{b}"])
                P.op("dve", lambda e: e.reciprocal(out=rs[b][0:rows, :], in_=sq[b][0:rows, :]),
                     reads=[f"sq{b}"], writes=[f"rs{b}"])
                P.op("dve", lambda e: e.tensor_scalar(
                    out=xn[b][0:rows, :], in0=xt[b][0:rows, :], scalar1=mv[b][0:rows, 0:1],
                    scalar2=rs[b][0:rows, 0:1], op0=ALU.subtract, op1=ALU.mult),
                    reads=[f"xt{b}", f"mv{b}", f"rs{b}"], writes=[f"xn{b}"])

            def p1B(ti):
                b = ti % 2
                rows = NMETA if ti == 0 else 128
                c0 = 0 if ti == 0 else NMETA + (ti - 1) * 128
                for kc in range(NKC):
                    P.op("pe", lambda e, kc=kc: e.transpose(
                        out=tp[b][:, kc, 0:rows], in_=xn[b][0:rows, kc * 128:(kc + 1) * 128],
                        identity=identf[0:rows, 0:rows]),
                        reads=[f"xn{b}", "cstf"], writes=[f"tp{b}"])
                for kc in range(NKC):
                    P.op("act", lambda e, kc=kc: e.activation(
                        out=hT[:, kc, c0:c0 + rows], in_=tp[b][:, kc, 0:rows], func=AF.Identity,
                        scale=embg(kc), bias=embb(kc)),
                        reads=[f"tp{b}", "pv"], writes=[f"hT{ti}"])

            p1A(0)
            for ti in range(NT + 1):
                if ti + 1 <= NT:
                    p1A(ti + 1)
                p1B(ti)
            P.op("act", lambda e: e.activation(out=scr1[:], in_=epsc[:], func=AF.Identity,
                                               scale=1.0, bias=zeroc[:]),
                 reads=["epsc", "zeroc"] + [f"hT{ti}" for ti in range(NT + 1)], writes=["hT", "scr1"])
            P.emit()

        with contextlib.ExitStack() as s2:
            sb2 = lambda name, shape, dt: s2.enter_context(nc.sbuf_tensor(name, shape, dt))
            ps2 = lambda name, shape, dt: s2.enter_context(nc.psum_tensor(name, shape, dt))
            z_ps = ps2("z_ps", [128, 2, 512], F32)
            incl_ps = ps2("incl_ps", [128, 2, 512], F32)
            o_ps = [ps2(f"o_ps{i}", [128, 512], F32) for i in range(2)]
            pj = [ps2(f"pj{i}", [128, 512], F32) for i in range(2)]
            wbf = [[sb2(f"wbf{s_}{i}", [128, NKC, 128], BF16) for i in range(4)] for s_ in range(2)]
            kT = sb2("kT", [128, LTOT], BF16)
            vtok = sb2("vtok", [128, NT + 1, 128], BF16)
            qz = [[sb2(f"qz{i}{h}", [128, 512], BF16) for h in range(2)] for i in range(2)]
            za = [sb2(f"za{i}", [128, 512], F32) for i in range(2)]
            zt = sb2("zt", [128, 512], F32)
            e_ = [sb2(f"e{i}", [128, 2, 512], F32) for i in range(3)]
            E2s = sb2("E2s", [128, 2, 512], F32)
            E2 = [E2s, E2s]
            sp_ = [sb2(f"sp{i}", [128, 2, 512], BF16) for i in range(2)]
            Ls = [sb2(f"Ls{i}", [128, 2, 512], BF16) for i in range(2)]
            A_ = [sb2(f"A{i}", [128, 2, 512], BF16) for i in range(2)]
            cb16 = sb2("cb16", [128, 5 * 128], BF16)
            tri16_b = cb16[:, 512:640]
            ident_b = cb16[:, 0:128]
            Tinc = cb16[:, 128:256]
            ones_b = cb16[:, 256:384]
            negm_b = cb16[:, 384:512]
            P.op("dve", lambda e: e.tensor_copy(out=cb16[:], in_=cstf[:, 0:640]), reads=["cstf"], writes=["cb16"])
            P.op("pool", lambda e: e.memset(vtok[:, 0, :], 0.0), writes=["vtm"])
            for i in range(2):
                P.op("pool", lambda e, i=i: e.memset(sp_[i][:], 0.0), writes=[f"sp{i}"])
                P.op("pool", lambda e, i=i: e.memset(A_[i][:], 0.0), writes=[f"A{i}"])
            for i in range(2):
                for h in range(2):
                    P.op("pool", lambda e, i=i, h=h: e.memset(qz[i][h][:], 0.0), writes=[f"qT{i}"])
            BK = [o_ps[0], o_ps[1], pj[0], pj[1]]
            bkctr = [0]

            def proj_fm(dst_ps, wt, cols, ncols, wkey, pkey):
                for kc in range(NKC):
                    P.op("pe", lambda e, kc=kc: e.matmul(
                        dst_ps[:, 0:ncols], lhsT=wt[:, kc, :], rhs=hT[:, kc, cols:cols + ncols],
                        start=(kc == 0), stop=(kc == NKC - 1)),
                        reads=[wkey, "hT"], writes=[pkey])

            npairs = NKC if stage >= 2 else 0
            if stage == 2:
                npairs = 1
            for hp in range(npairs):
                ws = hp % 2
                for j in range(4):
                    u = j * 8 + hp
                    P.op("pool", lambda e, j=j, u=u, ws=ws: e.dma_start(out=wbf[ws][j][:], in_=w_in_u[u, :, :, :]),
                         writes=[f"wbf{ws}{j}"], dma=f"dwbf{ws}{j}")
                wq, wk, wv, wz = wbf[ws]
                kq, kk, kv, kz = [f"wbf{ws}{j}" for j in range(4)]
                def kv_tasks(g, bK, bV):
                    kkey = "kTm" if g < 0 else f"kT{g}"
                    vkey = "vtm" if g < 0 else f"vt{g}"
                    ncols = NMETA if g < 0 else 512
                    cols = 0 if g < 0 else NMETA + g * 512
                    tasks = []
                    tasks.append(lambda: proj_fm(BK[bK], wk, cols, ncols, kk, f"bk{bK}"))
                    tasks.append(lambda: P.op("dve", lambda e: e.tensor_copy(
                        out=kT[:, cols:cols + ncols], in_=BK[bK][:, 0:ncols]),
                        reads=[f"bk{bK}"], writes=[kkey]))
                    tiles = [0] if g < 0 else list(range(4 * g + 1, 4 * g + 5))
                    rows = NMETA if g < 0 else 128

                    def vtile(si, t):
                        c0 = 0 if t == 0 else NMETA + (t - 1) * 128
                        for kc in range(NKC):
                            P.op("pe", lambda e, kc=kc, wv=wv: e.matmul(
                                BK[bV][0:rows, si * 128:(si + 1) * 128], lhsT=hT[:, kc, c0:c0 + rows],
                                rhs=wv[:, kc, :], start=(kc == 0), stop=(kc == NKC - 1)),
                                reads=[kv, "hT"], writes=[f"bk{bV}"])
                    for si, t in enumerate(tiles):
                        tasks.append(lambda si=si, t=t: vtile(si, t))
                    nt_ = len(tiles)
                    tasks.append(lambda: P.op("dve", lambda e: e.tensor_copy(
                        out=vtok[0:rows, tiles[0]:tiles[0] + nt_, :],
                        in_=BK[bV][0:rows, 0:nt_ * 128].rearrange("p (a b) -> p a b", b=128)),
                        reads=[f"bk{bV}"], writes=[vkey]))
                    return tasks

                def emit_kv(g, bK, bV):
                    for tk in kv_tasks(g, bK, bV):
                        tk()

                emit_kv(-1, 2, 3)
                emit_kv(0, 2, 3)

                def emit_group_proj_pe(g, part=None):
                    cols = NMETA + g * 512
                    p0 = 2 * (g % 2)
                    if part in (None, 0):
                        proj_fm(BK[p0], wq, cols, 512, kq, f"bk{p0}")
                    if part in (None, 1):
                        proj_fm(BK[p0 + 1], wz, cols, 512, kz, f"bk{p0 + 1}")

                def group_tasks(g1):
                    b0 = 2 * (g1 % 2)
                    return ([lambda: emit_group_proj_pe(g1, 0), lambda: emit_group_proj_pe(g1, 1),
                             lambda: emit_group_proj_post(g1)] + kv_tasks(g1, b0, b0 + 1))

                def emit_group_proj_post(g):
                    gb = g % 2
                    p0 = 2 * (g % 2)
                    bq, bz = BK[p0], BK[p0 + 1]
                    kbq, kbz = f"bk{p0}", f"bk{p0 + 1}"
                    for h in range(2):
                        P.op("dve", lambda e, h=h: e.tensor_scalar(
                            out=qz[gb][h][64 * h:64 * h + 64, :], in0=bq[64 * h:64 * h + 64, :], scalar1=0.125,
                            scalar2=None, op0=ALU.mult), reads=[kbq], writes=[f"qT{gb}"])
                    P.op("act", lambda e: e.activation(out=zt[:], in_=bz[:], func=AF.Exp, scale=-1.0, bias=zeroc[:]),
                         reads=[kbz, "zeroc"], writes=["zt"])
                    P.op("act", lambda e: e.activation(out=zt[:], in_=zt[:], func=AF.Ln, scale=1.0, bias=onec[:]),
                         reads=["zt", "onec"], writes=["zt"])
                    P.op("act", lambda e: e.activation(out=zt[:], in_=zt[:], func=AF.Exp, scale=-1.0, bias=zeroc[:]),
                         reads=["zt", "zeroc"], writes=["zt"])
                    P.op("dve", lambda e: e.tensor_tensor(out=za[gb][:], in0=bz[:], in1=zt[:], op=ALU.mult),
                         reads=[kbz, "zt"], writes=[f"za{gb}"])

                steps = []
                for g in range(NG):
                    kbs = list(range(4 * g + 3, -1, -1)) + [-1]
                    for si, kb in enumerate(kbs):
                        diag = kb >= 4 * g
                        c0 = (kb - 4 * g) * 128 if diag else 0
                        steps.append(dict(g=g, kb=kb, P=(NMETA if kb < 0 else 128), c0=c0, diag=diag,
                                          first=(si == 0), last=(si == len(kbs) - 1), si=si, ns=len(kbs)))
                n = len(steps)

                def QK(i):
                    s_ = steps[i]
                    Pn, c0, g = s_["P"], s_["c0"], s_["g"]
                    kc0 = 0 if s_["kb"] < 0 else NMETA + s_["kb"] * 128
                    dg = s_["diag"]
                    kkey_ = "kTm" if s_["kb"] < 0 else f"kT{s_['kb'] // 4}"
                    for h in range(2):
                        P.op("pe", lambda e, h=h: e.matmul(
                            z_ps[0:Pn, h, c0:512], lhsT=kT[:, kc0:kc0 + Pn],
                            rhs=qz[g % 2][h][:, c0:512], start=True, stop=(not dg)),
                            reads=[kkey_, f"qT{g % 2}"], writes=["z_ps"])
                        if dg:
                            P.op("pe", lambda e, h=h: e.matmul(
                                z_ps[0:128, h, c0:c0 + 128], lhsT=ident_b, rhs=negm_b, start=False, stop=True),
                                reads=["cb16"], writes=["z_ps"])

                emit_group_proj_pe(0)
                emit_group_proj_post(0)
                QK(0)
                for i in range(0, n + 1):
                    b = i % 2
                    pb_ = (i - 1) % 2
                    eb = i % 3
                    epb = (i - 1) % 3
                    s_ = steps[i] if i < n else None
                    sp_prev = steps[i - 1] if i >= 1 else None
                    if s_ is not None:
                        Pn, c0 = s_["P"], s_["c0"]
                        P.op("act", lambda e, eb=eb, Pn=Pn, c0=c0: e.activation(
                            out=e_[eb][0:Pn, :, c0:512], in_=z_ps[0:Pn, :, c0:512], func=AF.Exp,
                            scale=1.0, bias=zeroc[0:Pn, :]), reads=["z_ps", "zeroc"], writes=[f"e{eb}"])
                        if s_["g"] + 1 < NG:
                            if s_["si"] == 0:
                                gq = group_tasks(s_["g"] + 1)
                            elif s_["last"]:
                                while gq:
                                    tk = gq.pop(0)
                                    if tk is not None:
                                        tk()
                            elif gq:
                                tk = gq.pop(0)
                                if tk is not None:
                                    tk()
                    if i + 1 < n:
                        QK(i + 1)
                    if s_ is not None:
                        P.op("act", lambda e, b=b, eb=eb, Pn=Pn, c0=c0: e.activation(
                            out=sp_[b][0:Pn, :, c0:512], in_=e_[eb][0:Pn, :, c0:512], func=AF.Ln,
                            scale=1.0, bias=onec[0:Pn, :]), reads=[f"e{eb}", "onec"], writes=[f"sp{b}"])
                    if sp_prev is not None:
                        Pp, cp = sp_prev["P"], sp_prev["c0"]
                        P.op("act", lambda e, pb_=pb_, Pp=Pp, cp=cp: e.activation(
                            out=E2[pb_][0:Pp, :, cp:512], in_=incl_ps[0:Pp, :, cp:512], func=AF.Exp,
                            scale=-1.0, bias=zeroc[0:Pp, :]), reads=["incl_ps", "zeroc"], writes=["E2s"])
                    if s_ is not None:
                        Pn, c0 = s_["P"], s_["c0"]
                        c0o = c0 + 128 if s_["diag"] else 0
                        lc, ln_ = f"Ls{b}", f"Ls{1 - b}"
                        has_old = c0o < 512
                        for h in range(2):
                            lt = Tinc[:, :] if Pn == 128 else tri16_b[:, 0:Pn]
                            P.op("pe", lambda e, b=b, Pn=Pn, c0=c0, h=h, has_old=has_old, lt=lt: e.matmul(
                                incl_ps[0:Pn, h, c0:512], lhsT=lt, rhs=sp_[b][:, h, c0:512],
                                start=True, stop=(not has_old)),
                                reads=[f"sp{b}", "cb16"], writes=["incl_ps"])
                            if has_old:
                                P.op("pe", lambda e, b=b, Pn=Pn, c0o=c0o, h=h: e.matmul(
                                    incl_ps[0:Pn, h, c0o:512], lhsT=ones_b[:, 0:Pn], rhs=Ls[b][:, h, c0o:512],
                                    start=False, stop=True),
                                    reads=[lc, "cb16"], writes=["incl_ps"])
                        if not s_["last"]:
                            if has_old:
                                P.op("dve", lambda e, b=b, c0o=c0o: e.tensor_tensor(
                                    out=Ls[1 - b][:, :, c0o:512], in0=Ls[b][:, :, c0o:512],
                                    in1=sp_[b][:, :, c0o:512], op=ALU.add),
                                    reads=[lc, f"sp{b}"], writes=[ln_])
                            if s_["diag"]:
                                P.op("dve", lambda e, b=b, c0=c0: e.tensor_copy(
                                    out=Ls[1 - b][:, :, c0:c0 + 128], in_=sp_[b][:, :, c0:c0 + 128]),
                                    reads=[f"sp{b}"], writes=[ln_])
                    if sp_prev is not None:
                        Pp, cp = sp_prev["P"], sp_prev["c0"]
                        P.op("dve", lambda e, pb_=pb_, epb=epb, Pp=Pp, cp=cp: e.tensor_tensor(
                            out=A_[pb_][0:Pp, :, cp:512], in0=e_[epb][0:Pp, :, cp:512],
                            in1=E2[pb_][0:Pp, :, cp:512], op=ALU.mult),
                            reads=[f"e{epb}", "E2s"], writes=[f"A{pb_}"])
                        vtile = 0 if sp_prev["kb"] < 0 else sp_prev["kb"] + 1
                        vkey_ = "vtm" if sp_prev["kb"] < 0 else f"vt{sp_prev['kb'] // 4}"
                        for h in range(2):
                            ob = 2 * (sp_prev["g"] % 2) + h
                            P.op("pe", lambda e, pb_=pb_, Pp=Pp, cp=cp, h=h, vtile=vtile, f_=sp_prev["first"],
                                 l_=sp_prev["last"], ob=ob: e.matmul(
                                BK[ob][:, cp:512], lhsT=vtok[:, vtile, :], rhs=A_[pb_][:, h, cp:512],
                                start=f_, stop=l_, skip_group_check=True),
                                reads=[f"A{pb_}", vkey_], writes=[f"bk{ob}"])
                        if sp_prev["last"]:
                            g = sp_prev["g"]
                            for h in range(2):
                                ob = 2 * (g % 2) + h
                                P.op("dve", lambda e, h=h, g=g, hp=hp, ob=ob: e.tensor_tensor(
                                    out=ogT[64 * h:64 * h + 64, hp, g * 512:(g + 1) * 512],
                                    in0=BK[ob][64 * h:64 * h + 64, :], in1=za[g % 2][64 * h:64 * h + 64, :],
                                    op=ALU.mult), reads=[f"bk{ob}", f"za{g % 2}"], writes=[f"ogT{g}"])
            P.emit()
        if dbg:
            P.op("sp", lambda e: e.dma_start(out=dbg_hT[:, :, :], in_=hT[:]),
                 reads=["hT"], writes=["dbg_hT"], dma="o0")
            P.final_waits("sp", ["o0"])
            P.emit()
        s12.close()


        nblk = NG if stage >= 3 else 0
        with contextlib.ExitStack() as s3:
            sb3 = lambda name, shape, dt: s3.enter_context(nc.sbuf_tensor(name, shape, dt))
            ps3 = lambda name, shape, dt: s3.enter_context(nc.psum_tensor(name, shape, dt))
            tp3 = ps3("tp3", [128, NKC, 128], F32)
            pp = [ps3(f"pp{i}", [128, 512], F32) for i in range(4)]
            stp = [ps3(f"stp{i}", [128, 512], F32) for i in range(2)]
            xt = [sb3(f"x3t{i}", [128, D], F32) for i in range(2)]
            st = sb3("st3", [128, 2, 6], F32)
            st1 = sb3("st3b", [128, 2, 6], F32)
            xs = [sb3(f"x3s{i}", [128, D], F32) for i in range(2)]
            mv = sb3("mv3", [128, 2], F32)
            mvs = sb3("mvs", [128, 4, 2], F32)
            rss = sb3("rss", [128, 4], F32)
            rs5 = sb3("rs5", [128, 1], F32)
            hTb = sb3("hTb", [128, NKC, NMETA + 512], BF16)
            ub = sb3("ub", [128, NKC, 30 + 512], BF16)
            diag = sb3("diag", [128, CONVK, 128], BF16)
            c_sb = sb3("c_sb", [128, NKC, 512], F32)
            T = [sb3(f"T{i}", [128, 512], F32) for i in range(7)]
            ycT = sb3("ycT", [128, NKC, 512], BF16)
            hmT = sb3("hmT", [128, NKC, 512], BF16)
            ro = [sb3(f"ro{i}", [128, D], F32) for i in range(2)]
            NWB = 10
            wb3 = [sb3(f"w3b{i}", [128, NKC, 128], BF16) for i in range(NWB)]
            wo0 = sb3("wo0", [128, NKC, 512], BF16)
            gb = [sb3(f"gb{i}", [128, D], F32) for i in range(4)]
            ncgb = sb3("ncgb", [128, 2 * NKC], F32)
            wo1 = ub[:, :, 30:542]
            onesf = cstf[:, 256:384]
            dwb = lambda cc: pv[:, 2 * NKC + cc:2 * NKC + cc + 1]
            cg = lambda cc: pv[:, 3 * NKC + cc:3 * NKC + cc + 1]
            cbi = lambda cc: pv[:, 4 * NKC + cc:4 * NKC + cc + 1]
            ncg = lambda cc: ncgb[:, cc:cc + 1]
            ncb = lambda cc: ncgb[:, NKC + cc:NKC + cc + 1]

            if nblk:
                for i in range(4):
                    P.op("sp", lambda e, i=i: e.dma_start(out=gb[i][:], in_=gbb[i, :, :]), writes=[f"gb{i}"], dma=f"gb{i}")
                for i in range(2):
                    P.op("dve", lambda e, i=i: e.tensor_scalar(out=gb[i][:], in0=gb[i][:], scalar1=ALPHA, scalar2=None,
                                                               op0=ALU.mult), reads=[f"gb{i}"], writes=[f"gb{i}"])
                P.op("dve", lambda e: e.tensor_scalar(out=ncgb[:], in0=pv[:, 3 * NKC:5 * NKC], scalar1=-1.0, scalar2=None,
                                                      op0=ALU.mult), reads=["pv"], writes=["ncgb"])
                P.op("pool", lambda e: e.memset(ub[:, :, 0:30], 0.0), writes=[f"ub{cc}" for cc in range(NKC)])

            wctr = [0]
            pend = []
            LAG = 4

            def wload(src_ap, dst_ap=None, dst_key=None):
                if dst_ap is None:
                    bi = wctr[0] % NWB
                    wctr[0] += 1
                    dst_ap, dst_key = wb3[bi][:], f"w3b{bi}"
                    ret = (wb3[bi], dst_key)
                    dk = dst_key
                else:
                    ret = (None, dst_key)
                    dk = dst_key if isinstance(dst_key, str) else dst_key[0]
                dkeys = dst_key if isinstance(dst_key, list) else [dst_key]
                P.op("pool", lambda e: e.dma_start(out=dst_ap, in_=src_ap), writes=dkeys, dma="d" + dk)
                return ret

            def flush():
                while pend:
                    pend.pop(0)()

            def item(specs, fn):
                tiles = [wload(*sp) for sp in specs]
                pend.append(lambda: fn(*tiles))
                while len(pend) > LAG:
                    pend.pop(0)()

            def proj3(dst_ps, pkey, wt, wkey, rhs_fn, rkeys, ncols=512, c0=0):
                for kc in range(NKC):
                    P.op("pe", lambda e, kc=kc, rhs=rhs_fn(kc): e.matmul(dst_ps[:, c0:c0 + ncols], lhsT=wt[:, kc, :], rhs=rhs,
                                                         start=(kc == 0), stop=(kc == NKC - 1)),
                         reads=[wkey] + rkeys, writes=[pkey])

            def sigm(dst, dkey, src, skey, ncols=512, c0=0, scale=None, bias=None, extra=()):
                d = dst[:, c0:c0 + ncols]
                P.op("act", lambda e: e.activation(out=d, in_=src, func=AF.Exp,
                                                   scale=(-1.0 if scale is None else scale),
                                                   bias=(zeroc[:] if bias is None else bias)),
                     reads=[skey, "zeroc"] + list(extra), writes=[dkey])
                P.op("act", lambda e: e.activation(out=d, in_=d, func=AF.Ln, scale=1.0, bias=onec[:]),
                     reads=[dkey, "onec"], writes=[dkey])
                P.op("act", lambda e: e.activation(out=d, in_=d, func=AF.Exp, scale=-1.0, bias=zeroc[:]),
                     reads=[dkey, "zeroc"], writes=[dkey])

            def ln_rstd(dst, dkey, var_ap, vkey, rows=128):
                P.op("act", lambda e: e.activation(out=dst, in_=var_ap, func=AF.Ln, scale=1.0, bias=epsc[0:rows, :]),
                     reads=[vkey, "epsc"], writes=[dkey])
                P.op("act", lambda e: e.activation(out=dst, in_=dst, func=AF.Exp, scale=-0.5, bias=zeroc[0:rows, :]),
                     reads=[dkey, "zeroc"], writes=[dkey])

            hblk = lambda kc: hTb[:, kc, NMETA:NMETA + 512]
            hmeta = lambda kc: hTb[:, kc, 0:NMETA]

            def s1_tile(blk, t, xbuf, xk, dq):
                rows = NMETA if t < 0 else 128
                src = meta[:, :] if t < 0 else x[blk * 512 + t * 128:blk * 512 + (t + 1) * 128, :]
                c0 = 0 if t < 0 else NMETA + t * 128
                P.op(dq, lambda e: e.dma_start(out=xbuf[0:rows, :], in_=src), writes=[xk], dma=xk)
                for h in range(2):
                    P.op("dve", lambda e, h=h: e.bn_stats(
                        out=st1[0:rows, h, :], in_=xbuf[0:rows, h * 512:(h + 1) * 512]),
                        reads=[xk], writes=["st1"])
                if t < 0:
                    mva, mkey, rsa, rkey = mv[0:rows, :], "mv3", rs5[0:rows, :], "rs5"
                else:
                    mva, mkey, rsa, rkey = mvs[:, t, :], f"mvs{t}", rss[:, t:t + 1], f"rss{t}"
                P.op("dve", lambda e: e.bn_aggr(out=mva, in_=st1[0:rows, :, :]), reads=["st1"], writes=[mkey])
                ln_rstd(rsa, rkey, mva[:, 1:2], mkey, rows)
                P.op("dve", lambda e: e.tensor_scalar(
                    out=xbuf[0:rows, :], in0=xbuf[0:rows, :], scalar1=mva[:, 0:1], scalar2=rsa,
                    op0=ALU.subtract, op1=ALU.mult), reads=[xk, mkey, rkey], writes=[xk])
                for kc in range(NKC):
                    P.op("pe", lambda e, kc=kc: e.transpose(
                        out=tp3[:, kc, 0:rows], in_=xbuf[0:rows, kc * 128:(kc + 1) * 128],
                        identity=identf[0:rows, 0:rows]), reads=[xk, "cstf"], writes=["tp3"])
                for kc in range(NKC):
                    P.op("act", lambda e, kc=kc: e.activation(
                        out=hTb[:, kc, c0:c0 + rows], in_=tp3[:, kc, 0:rows], func=AF.Identity,
                        scale=embg(kc), bias=embb(kc)), reads=["tp3", "pv"], writes=["hTb"])

            def s1_body(blk):
                tiles = ([-1] if blk == 0 else []) + list(range(4))
                for t in tiles:
                    xb = (t + 1) % 2
                    s1_tile(blk, t, xt[xb], f"x3t{xb}", "sp")

            def s2a_body(blk, cc, wuk, wgk):
                (wu, ku), (wg, kg) = wuk, wgk
                proj3(pp[0], "pp0", wu, ku, hblk, ["hTb"])
                proj3(pp[1], "pp1", wg, kg, hblk, ["hTb"])
                if blk == 0:
                    proj3(pp[2], "pp2", wu, ku, hmeta, ["hTb"], ncols=NMETA, c0=0)
                    proj3(pp[2], "pp2", wg, kg, hmeta, ["hTb"], ncols=NMETA, c0=NMETA)
                    sigm(T[1], "T1", pp[2][:, NMETA:2 * NMETA], "pp2", ncols=NMETA)
                    P.op("dve", lambda e: e.tensor_tensor(out=ub[:, cc, 14:30], in0=pp[2][:, 0:NMETA],
                                                          in1=T[1][:, 0:NMETA], op=ALU.mult),
                         reads=["pp2", "T1"], writes=[f"ub{cc}"])
                sigm(T[0], "T0", pp[1][:], "pp1")
                P.op("dve", lambda e: e.tensor_tensor(out=ub[:, cc, 30:542], in0=pp[0][:], in1=T[0][:],
                                                      op=ALU.mult), reads=["pp0", "T0"], writes=[f"ub{cc}"])

            def s2b_body(cc):
                dwo = 5 * NKC + cc * CONVK
                P.op("pool", lambda e: e.tensor_tensor(
                    out=diag[:], in0=cstf[:, 0:128].unsqueeze(1).broadcast_to([128, CONVK, 128]),
                    in1=pv[:, dwo:dwo + CONVK].unsqueeze(2).broadcast_to([128, CONVK, 128]), op=ALU.mult),
                    reads=["cstf", "pv"], writes=["diag"])
                for k in range(CONVK):
                    P.op("pe", lambda e, k=k: e.matmul(pp[3][:], lhsT=diag[:, k, :], rhs=ub[:, cc, k:k + 512],
                                                       start=(k == 0), stop=(k == CONVK - 1)),
                         reads=["diag", f"ub{cc}"], writes=["pp3"])
                P.op("act", lambda e: e.activation(out=c_sb[:, cc, :], in_=pp[3][:], func=AF.Identity,
                                                   scale=1.0, bias=dwb(cc)),
                     reads=["pp3", "pv"], writes=[f"c_sb{cc}"])
                tq = T[2 + cc % 2]
                tqk = f"T{2 + cc % 2}"
                P.op("act", lambda e: e.activation(out=tq[:], in_=c_sb[:, cc, :], func=AF.Square,
                                                   scale=1.0, bias=zeroc[:]),
                     reads=[f"c_sb{cc}", "zeroc"], writes=[tqk])

            def s2c_body(cc):
                tq = T[2 + cc % 2]
                tqk = f"T{2 + cc % 2}"
                P.op("pe", lambda e: e.matmul(stp[0][:], lhsT=onesf, rhs=c_sb[:, cc, :],
                                              start=(cc == 0), stop=(cc == NKC - 1)),
                     reads=["cstf", f"c_sb{cc}"], writes=["stp0"])
                P.op("pe", lambda e: e.matmul(stp[1][:], lhsT=onesf, rhs=tq[:],
                                              start=(cc == 0), stop=(cc == NKC - 1)),
                     reads=["cstf", tqk], writes=["stp1"])

            def s3pre_body():
                P.op("pool", lambda e: e.tensor_copy(out=ub[:, :, 0:30], in_=ub[:, :, 512:542]),
                     reads=[f"ub{cc}" for cc in range(NKC)], writes=[f"ub{cc}" for cc in range(NKC)])
                P.op("dve", lambda e: e.tensor_scalar(out=T[2][:], in0=stp[0][:], scalar1=1.0 / D, scalar2=None,
                                                      op0=ALU.mult), reads=["stp0"], writes=["T2"])
                P.op("dve", lambda e: e.tensor_tensor(out=T[4][:], in0=T[2][:], in1=T[2][:], op=ALU.mult),
                     reads=["T2"], writes=["T4"])
                P.op("dve", lambda e: e.scalar_tensor_tensor(out=T[3][:], in0=stp[1][:], scalar=1.0 / D, in1=T[4][:],
                                                             op0=ALU.mult, op1=ALU.subtract),
                     reads=["stp1", "T4"], writes=["T3"])
                ln_rstd(T[3][:], "T3", T[3][:], "T3")

            def s3_body(cc, wzk):
                wz, kz = wzk
                proj3(pp[0], "pp0", wz, kz, hblk, ["hTb"])
                P.op("dve", lambda e: e.tensor_tensor(out=T[0][:], in0=c_sb[:, cc, :], in1=T[2][:],
                                                      op=ALU.subtract),
                     reads=[f"c_sb{cc}", "T2"], writes=["T0"])
                P.op("dve", lambda e: e.tensor_tensor(out=T[0][:], in0=T[0][:], in1=T[3][:], op=ALU.mult),
                     reads=["T0", "T3"], writes=["T0"])
                P.op("dve", lambda e: e.tensor_scalar(out=T[1][:], in0=T[0][:], scalar1=cg(cc), scalar2=cbi(cc),
                                                      op0=ALU.mult, op1=ALU.add),
                     reads=["T0", "pv"], writes=["T1"])
                sigm(T[5], "T5", T[0][:], "T0", scale=ncg(cc), bias=ncb(cc), extra=["ncgb"])
                sigm(T[4], "T4", pp[0][:], "pp0")
                P.op("dve", lambda e: e.tensor_tensor(out=T[1][:], in0=T[1][:], in1=T[5][:], op=ALU.mult),
                     reads=["T1", "T5"], writes=["T1"])
                P.op("dve", lambda e: e.tensor_tensor(out=T[1][:], in0=T[1][:], in1=pp[0][:], op=ALU.mult),
                     reads=["T1", "pp0"], writes=["T1"])
                P.op("dve", lambda e: e.tensor_tensor(out=ycT[:, cc, :], in0=T[1][:], in1=T[4][:],
                                                      op=ALU.mult),
                     reads=["T1", "T4"], writes=["ycT"])

            def s4b_body(blk, m, waok, wgak):
                (wao, kao), (wga, kga) = waok, wgak
                proj3(pp[2], "pp2", wao, kao, lambda kc: ogT[:, kc, blk * 512:(blk + 1) * 512], [f"ogT{blk}"])
                proj3(pp[3], "pp3", wga, kga, hblk, ["hTb"])
                sigm(T[6], "T6", pp[3][:], "pp3")
                P.op("dve", lambda e: e.tensor_tensor(out=c_sb[:, m, :], in0=pp[2][:], in1=T[6][:], op=ALU.mult),
                     reads=["pp2", "T6"], writes=[f"c_sb{m}"])

            def s4a_body(m, wcok, wgck):
                (wco, kco), (wgc, kgc) = wcok, wgck
                q = m % 2
                pa, pb, ka, kb_ = pp[2 * q], pp[2 * q + 1], f"pp{2 * q}", f"pp{2 * q + 1}"
                ta, tb, kta, ktb = T[q], T[4 + q], f"T{q}", f"T{4 + q}"
                proj3(pa, ka, wco, kco, lambda kc: ycT[:, kc, :], ["ycT"])
                proj3(pb, kb_, wgc, kgc, hblk, ["hTb"])
                sigm(ta, kta, pb[:], kb_)
                P.op("dve", lambda e: e.tensor_tensor(out=tb[:], in0=pa[:], in1=ta[:], op=ALU.mult),
                     reads=[ka, kta], writes=[ktb])
                P.op("dve", lambda e: e.tensor_tensor(out=hmT[:, m, :], in0=tb[:], in1=c_sb[:, m, :], op=ALU.add),
                     reads=[ktb, f"c_sb{m}"], writes=["hmT"])

            wo1keys = [f"ub{cc}" for cc in range(NKC)] + [f"wo1_{c}" for c in range(4)]
            wo0keys = [f"wo0_{c}" for c in range(4)]

            def s5_body(blk, *_):
                def partA(t):
                    xb = t % 2
                    xk = f"x3t{xb}"
                    src = x[blk * 512 + t * 128:blk * 512 + (t + 1) * 128, :]
                    P.op("act", lambda e: e.dma_start(out=xt[xb][:], in_=src), writes=[xk], dma=xk)
                    P.op("dve", lambda e: e.tensor_scalar(
                        out=xt[xb][:], in0=xt[xb][:], scalar1=mvs[:, t, 0:1], scalar2=rss[:, t:t + 1],
                        op0=ALU.subtract, op1=ALU.mult), reads=[xk, f"mvs{t}", f"rss{t}"], writes=[xk])
                    P.op("dve", lambda e: e.tensor_tensor(out=xt[xb][:], in0=xt[xb][:], in1=gb[0][:], op=ALU.mult),
                         reads=[xk, "gb0"], writes=[xk])
                    P.op("dve", lambda e: e.tensor_tensor(out=xt[xb][:], in0=xt[xb][:], in1=gb[1][:], op=ALU.add),
                         reads=[xk, "gb1"], writes=[xk])

                partA(0)
                for t in range(4):
                    xb = t % 2
                    rb = t % 2
                    xk = f"x3t{xb}"
                    pa, pb = (pp[0], pp[1]) if t % 2 == 0 else (pp[2], pp[3])
                    pak, pbk = ("pp0", "pp1") if t % 2 == 0 else ("pp2", "pp3")
                    for m in range(NKC):
                        P.op("pe", lambda e, m=m, t=t, pa=pa: e.matmul(
                            pa[:], lhsT=hmT[:, m, t * 128:(t + 1) * 128], rhs=wo0[:, m, :],
                            start=(m == 0), stop=(m == NKC - 1)), reads=["hmT"] + wo0keys, writes=[pak])
                    for m in range(NKC):
                        P.op("pe", lambda e, m=m, t=t, pb=pb: e.matmul(
                            pb[:], lhsT=hmT[:, m, t * 128:(t + 1) * 128], rhs=wo1[:, m, :],
                            start=(m == 0), stop=(m == NKC - 1)), reads=["hmT"] + wo1keys, writes=[pbk])
                    if t + 1 < 4:
                        partA(t + 1)
                    P.op("dve", lambda e, rb=rb, pa=pa, xb=xb: e.tensor_tensor(out=ro[rb][:, 0:512], in0=xt[xb][:, 0:512],
                                                                       in1=pa[:], op=ALU.add),
                         reads=[xk, pak], writes=[f"ro{rb}"])
                    P.op("dve", lambda e, rb=rb, pb=pb, xb=xb: e.tensor_tensor(out=ro[rb][:, 512:1024], in0=xt[xb][:, 512:1024],
                                                                       in1=pb[:], op=ALU.add),
                         reads=[xk, pbk], writes=[f"ro{rb}"])
                    for h in range(2):
                        P.op("dve", lambda e, rb=rb, h=h: e.bn_stats(out=st[:, h, :], in_=ro[rb][:, h * 512:(h + 1) * 512]),
                             reads=[f"ro{rb}"], writes=["st3"])
                    P.op("dve", lambda e: e.bn_aggr(out=mv[:], in_=st[:, :, :]), reads=["st3"], writes=["mv3"])
                    ln_rstd(rs5[:], "rs5", mv[:, 1:2], "mv3")
                    P.op("dve", lambda e, rb=rb: e.tensor_scalar(
                        out=ro[rb][:], in0=ro[rb][:], scalar1=mv[:, 0:1], scalar2=rs5[:],
                        op0=ALU.subtract, op1=ALU.mult), reads=[f"ro{rb}", "mv3", "rs5"], writes=[f"ro{rb}"])
                    P.op("pool", lambda e, rb=rb: e.tensor_tensor(out=ro[rb][:], in0=ro[rb][:], in1=gb[2][:], op=ALU.mult),
                         reads=[f"ro{rb}", "gb2"], writes=[f"ro{rb}"])
                    P.op("pool", lambda e, rb=rb: e.tensor_tensor(out=ro[rb][:], in0=ro[rb][:], in1=gb[3][:], op=ALU.add),
                         reads=[f"ro{rb}", "gb3"], writes=[f"ro{rb}"])
                    P.op("sp", lambda e, rb=rb, t=t: e.dma_start(
                        out=out[blk * 512 + t * 128:blk * 512 + (t + 1) * 128, :], in_=ro[rb][:]),
                        reads=[f"ro{rb}"], writes=["out_hbm"], dma=f"oro{rb}")
                    if blk + 1 < nblk:
                        s1_tile(blk + 1, t, xs[t % 2], f"x3s{t % 2}", "act")

            for blk in range(nblk):
                if blk == 0:
                    item([], lambda blk=blk: s1_body(blk))
                wsp = lambda cc: [(w_in_u[32 + cc, :, :, :],), (w_in_u[40 + cc, :, :, :],)]
                for cc in range(NKC + 2):
                    if cc < NKC:
                        item(wsp(cc), lambda a, b_, blk=blk, cc=cc: s2a_body(blk, cc, a, b_))
                    if 0 <= cc - 1 < NKC:
                        item([], lambda cc=cc: s2b_body(cc - 1))
                    if 0 <= cc - 2 < NKC:
                        item([], lambda cc=cc: s2c_body(cc - 2))
                item([], s3pre_body)
                for cc in range(NKC + 1):
                    if cc < NKC:
                        item([(w_in_u[48 + cc, :, :, :],)], lambda a, cc=cc: s3_body(cc, a))
                    if cc >= 1:
                        m = cc - 1
                        item([(w_ao_u[m, :, :, :],), (w_in_u[56 + m, :, :, :],)],
                             lambda a, b_, blk=blk, m=m: s4b_body(blk, m, a, b_))
                for m in range(NKC):
                    item([(w_co_u[m, :, :, :],), (w_in_u[64 + m, :, :, :],)], lambda a, b_, m=m: s4a_body(m, a, b_))
                specs = []
                for c in range(4):
                    specs.append((w_o_u[0, :, :, c * 128:(c + 1) * 128], wo0[:, :, c * 128:(c + 1) * 128], f"wo0_{c}"))
                for c in range(4):
                    specs.append((w_o_u[1, :, :, c * 128:(c + 1) * 128], wo1[:, :, c * 128:(c + 1) * 128],
                                  [f"wo1_{c}"] + [f"ub{cc}" for cc in range(NKC)]))
                item(specs, lambda *a, blk=blk: s5_body(blk, *a))
            flush()
            P.final_waits("sp", [k for k in P.cnt if k.startswith("o")])
            P.emit()

        if dbg:
            P.op("sp", lambda e: e.dma_start(out=dbg_og[:, :, :], in_=ogT[:]),
                 reads=[f"ogT{g}" for g in range(NG)], writes=["dbg_og"], dma="o0")
        P.final_waits("sp", [k for k in P.cnt if k.startswith("o")])
        P.emit()
    return nc


def _host_layout(inp, b):
    f = lambda a: np.ascontiguousarray(a, dtype=np.float32)
    pm = lambda v: f(np.asarray(v).reshape(NKC, 128).T)
    dw = np.asarray(inp["dw_w"][0])
    dwp = dw.T.reshape(NKC, 128, CONVK).transpose(1, 0, 2).reshape(128, NKC * CONVK)
    pvec = np.concatenate([pm(inp["emb_ln_g"]), pm(inp["emb_ln_b"]), pm(inp["dw_b"][0]),
                           pm(inp["conv_ln_g"][0]), pm(inp["conv_ln_b"][0]), f(dwp)], axis=1)
    gbb = np.stack([np.broadcast_to(np.asarray(v).reshape(1, D), (128, D)) for v in
                    (inp["emb_ln_g"], inp["emb_ln_b"], inp["post_ln_g"][0], inp["post_ln_b"][0])])

    def units(w, ncols):
        w = np.asarray(w)
        n = w.shape[1] // ncols
        return f(w.reshape(NKC, 128, n, ncols).transpose(2, 1, 0, 3))
    ii = np.arange(128)
    ident = (ii[:, None] == ii[None, :])
    tri = (ii[:, None] >= ii[None, :])
    ones = np.ones((128, 128), bool)
    negm = np.where(ii[:, None] >= ii[None, :], -30000.0, 0.0)
    tri16 = tri & (ii[:, None] < NMETA)
    cst = np.concatenate([ident, tri, ones, negm, tri16], axis=1).astype(np.float32)
    return {
        "x": f(inp["x"][b]), "meta": f(inp["meta_tokens"]), "pvec": f(pvec), "gbb": f(gbb),
        "w_in_u": units(inp["w_in"][0], 128), "w_ao_u": units(inp["w_attn_out"][0], 128),
        "w_co_u": units(inp["w_conv_out"][0], 128), "w_o_u": units(inp["w_out"][0], 512),
        "cst": f(cst),
    }


def kernel(**inputs):
    nc = build()
    shared = _host_layout(inputs, 0)
    in_maps = []
    for b in range(8):
        m = dict(shared)
        m["x"] = np.ascontiguousarray(inputs["x"][b], dtype=np.float32)
        in_maps.append(m)
    res = run_bass_kernel_spmd(nc, in_maps, core_ids=list(range(8)))
    return np.stack([np.asarray(r["out"]) for r in res.results], axis=0).astype(np.float32)
```

```python
import contextlib
import numpy as np
import concourse.bass as bass
import concourse.mybir as mybir
from concourse.bass_utils import run_bass_kernel_spmd

F32 = mybir.dt.float32
BF16 = mybir.dt.bfloat16
AF = mybir.ActivationFunctionType
ALU = mybir.AluOpType

D = 1024
SEQ = 4096
NMETA = 16
LTOT = SEQ + NMETA
NKC = 8
EPS = 1e-5
ALPHA = 2.0 ** 0.25
CONVK = 31
NT = SEQ // 128
NG = SEQ // 512


class Prog:
    ENGS = ("pe", "act", "dve", "pool", "sp")
    PSUM_PREFIX = ("pp", "tp", "stp", "z_ps", "incl_ps", "bk")

    def __init__(self, nc, stack):
        self.nc = nc
        self.stack = stack
        self.sem = {}
        for e in self.ENGS:
            self.sem[e] = stack.enter_context(nc.semaphore("s_" + e))
        self.cnt = {e: 0 for e in self.ENGS}
        self.known = {e: {} for e in self.ENGS}
        self.lastw = {}
        self.readers = {}
        self.streams = {e: [] for e in self.ENGS}

    def dma_sem(self, key):
        if key not in self.sem:
            self.sem[key] = self.stack.enter_context(self.nc.semaphore("d_" + key))
            self.cnt[key] = 0
        return key

    def op(self, eng, fn, reads=(), writes=(), dma=None):
        deps = {}

        def need(ev, kind):
            k, v = ev
            if k == eng:
                if eng == "pe" or kind != "raw":
                    return
            if deps.get(k, 0) < v:
                deps[k] = v

        for b in reads:
            ev = self.lastw.get(b)
            if ev is not None:
                need(ev, "raw")
            if b.startswith(self.PSUM_PREFIX):
                for ev in self.readers.get(b, ()):
                    need(ev, "war")
        for b in writes:
            ev = self.lastw.get(b)
            if ev is not None:
                need(ev, "waw")
            for ev in self.readers.get(b, ()):
                need(ev, "war")
        waits = []
        kn = self.known[eng]
        for k, v in deps.items():
            if kn.get(k, 0) < v:
                kn[k] = v
                waits.append((k, v))
        if dma is None:
            self.cnt[eng] += 1
            ev = (eng, self.cnt[eng])
            inc = 1
        else:
            self.dma_sem(dma)
            self.cnt[dma] += 16
            ev = (dma, self.cnt[dma])
            inc = 16
        self.streams[eng].append((waits, fn, ev[0], inc))
        for b in reads:
            self.readers.setdefault(b, []).append(ev)
        for b in writes:
            self.lastw[b] = ev
            self.readers[b] = []
        return ev

    def final_waits(self, eng, keys):
        waits = [(k, self.cnt[k]) for k in keys if self.cnt.get(k, 0) > 0]
        self.streams[eng].append((waits, None, None, 0))

    def emit(self):
        nc = self.nc
        with nc.Block() as block:
            def run(engh, name):
                for waits, fn, semk, inc in self.streams[name]:
                    for k, v in waits:
                        engh.wait_ge(self.sem[k], v)
                    if fn is not None:
                        fn(engh).then_inc(self.sem[semk], inc)

            @block.tensor
            def _(t):
                run(t, "pe")

            @block.scalar
            def _(t):
                run(t, "act")

            @block.vector
            def _(t):
                run(t, "dve")

            @block.gpsimd
            def _(t):
                run(t, "pool")

            @block.sync
            def _(t):
                run(t, "sp")
        self.streams = {e: [] for e in self.ENGS}


def build(stage=99, dbg=False):
    nc = bass.Bass("TRN2", target_bir_lowering=False)
    x = nc.dram_tensor("x", [SEQ, D], F32, kind="ExternalInput").ap()
    meta = nc.dram_tensor("meta", [NMETA, D], F32, kind="ExternalInput").ap()
    pvec = nc.dram_tensor("pvec", [128, 5 * NKC + NKC * CONVK], F32, kind="ExternalInput").ap()
    gbb = nc.dram_tensor("gbb", [4, 128, D], F32, kind="ExternalInput").ap()
    w_in_u = nc.dram_tensor("w_in_u", [72, 128, NKC, 128], F32, kind="ExternalInput").ap()
    w_ao_u = nc.dram_tensor("w_ao_u", [8, 128, NKC, 128], F32, kind="ExternalInput").ap()
    w_co_u = nc.dram_tensor("w_co_u", [8, 128, NKC, 128], F32, kind="ExternalInput").ap()
    w_o_u = nc.dram_tensor("w_o_u", [2, 128, NKC, 512], F32, kind="ExternalInput").ap()
    cst = nc.dram_tensor("cst", [128, 5 * 128], F32, kind="ExternalInput").ap()
    out = nc.dram_tensor("out", [SEQ, D], F32, kind="ExternalOutput").ap()
    dbg_hT = None
    if dbg:
        dbg_hT = nc.dram_tensor("dbg_hT", [128, NKC, LTOT], BF16, kind="ExternalOutput").ap()
        dbg_og = nc.dram_tensor("dbg_og", [128, NKC, SEQ], BF16, kind="ExternalOutput").ap()

    with contextlib.ExitStack() as stack:
        P = Prog(nc, stack)
        sb = lambda name, shape, dt: stack.enter_context(nc.sbuf_tensor(name, shape, dt))
        ps = lambda name, shape, dt: stack.enter_context(nc.psum_tensor(name, shape, dt))

        pv = sb("pv", [128, 5 * NKC + NKC * CONVK], F32)
        cstf = sb("cstf", [128, 5 * 128], F32)
        identf = cstf[:, 0:128]
        epsc = sb("epsc", [128, 1], F32)
        onec = sb("onec", [128, 1], F32)
        zeroc = sb("zeroc", [128, 1], F32)
        scr1 = sb("scr1", [128, 1], F32)
        ogT = sb("ogT", [128, NKC, SEQ], BF16)
        P.op("pool", lambda e: e.memset(epsc[:], EPS), writes=["epsc"])
        P.op("pool", lambda e: e.memset(onec[:], 1.0), writes=["onec"])
        P.op("pool", lambda e: e.memset(zeroc[:], 0.0), writes=["zeroc"])

        P.op("sp", lambda e: e.dma_start(out=pv[:], in_=pvec[:, :]), writes=["pv"], dma="c0")
        P.op("sp", lambda e: e.dma_start(out=cstf[:], in_=cst[:, :]), writes=["cstf"], dma="c1")
        embg = lambda kc: pv[:, kc:kc + 1]
        embb = lambda kc: pv[:, NKC + kc:NKC + kc + 1]

        s12 = contextlib.ExitStack()
        hT = s12.enter_context(nc.sbuf_tensor("hT", [128, NKC, LTOT], BF16))
        with contextlib.ExitStack() as s1:
            sb1 = lambda name, shape, dt: s1.enter_context(nc.sbuf_tensor(name, shape, dt))
            ps1 = lambda name, shape, dt: s1.enter_context(nc.psum_tensor(name, shape, dt))
            xt = [sb1(f"xt{i}", [128, D], F32) for i in range(2)]
            xn = [sb1(f"xn{i}", [128, D], F32) for i in range(2)]
            st = [sb1(f"st{i}", [128, 2, 6], F32) for i in range(2)]
            mv = [sb1(f"mv{i}", [128, 2], F32) for i in range(2)]
            rs = [sb1(f"rs{i}", [128, 1], F32) for i in range(2)]
            sq = [sb1(f"sq{i}", [128, 1], F32) for i in range(2)]
            tp = [ps1(f"tp{i}", [128, NKC, 128], F32) for i in range(2)]

            def p1A(ti):
                b = ti % 2
                rows = NMETA if ti == 0 else 128
                src = meta[:, :] if ti == 0 else x[(ti - 1) * 128:ti * 128, :]
                P.op("sp", lambda e: e.dma_start(out=xt[b][0:rows, :], in_=src), writes=[f"xt{b}"], dma=f"xt{b}")
                for h in range(2):
                    P.op("dve", lambda e, h=h: e.bn_stats(
                        out=st[b][0:rows, h, :], in_=xt[b][0:rows, h * 512:(h + 1) * 512]),
                        reads=[f"xt{b}"], writes=[f"st{b}{h}"])
                P.op("dve", lambda e: e.bn_aggr(out=mv[b][0:rows, :], in_=st[b][0:rows, :, :]),
                     reads=[f"st{b}0", f"st{b}1"], writes=[f"mv{b}"])
                P.op("act", lambda e: e.activation(
                    out=sq[b][0:rows, :], in_=mv[b][0:rows, 1:2], func=AF.Sqrt, bias=epsc[0:rows, :], scale=1.0),
                    reads=[f"mv{b}", "epsc"], writes=[f"sq{b}"])
                P.op("dve", lambda e: e.reciprocal(out=rs[b][0:rows, :], in_=sq[b][0:rows, :]),
                     reads=[f"sq{b}"], writes=[f"rs{b}"])
                P.op("dve", lambda e: e.tensor_scalar(
                    out=xn[b][0:rows, :], in0=xt[b][0:rows, :], scalar1=mv[b][0:rows, 0:1],
                    scalar2=rs[b][0:rows, 0:1], op0=ALU.subtract, op1=ALU.mult),
                    reads=[f"xt{b}", f"mv{b}", f"rs{b}"], writes=[f"xn{b}"])

            def p1B(ti):
                b = ti % 2
                rows = NMETA if ti == 0 else 128
                c0 = 0 if ti == 0 else NMETA + (ti - 1) * 128
                for kc in range(NKC):
                    P.op("pe", lambda e, kc=kc: e.transpose(
                        out=tp[b][:, kc, 0:rows], in_=xn[b][0:rows, kc * 128:(kc + 1) * 128],
                        identity=identf[0:rows, 0:rows]),
                        reads=[f"xn{b}", "cstf"], writes=[f"tp{b}"])
                for kc in range(NKC):
                    P.op("act", lambda e, kc=kc: e.activation(
                        out=hT[:, kc, c0:c0 + rows], in_=tp[b][:, kc, 0:rows], func=AF.Identity,
                        scale=embg(kc), bias=embb(kc)),
                        reads=[f"tp{b}", "pv"], writes=[f"hT{ti}"])

            p1A(0)
            for ti in range(NT + 1):
                if ti + 1 <= NT:
                    p1A(ti + 1)
                p1B(ti)
            P.op("act", lambda e: e.activation(out=scr1[:], in_=epsc[:], func=AF.Identity,
                                               scale=1.0, bias=zeroc[:]),
                 reads=["epsc", "zeroc"] + [f"hT{ti}" for ti in range(NT + 1)], writes=["hT", "scr1"])
            P.emit()

        with contextlib.ExitStack() as s2:
            sb2 = lambda name, shape, dt: s2.enter_context(nc.sbuf_tensor(name, shape, dt))
            ps2 = lambda name, shape, dt: s2.enter_context(nc.psum_tensor(name, shape, dt))
            z_ps = ps2("z_ps", [128, 2, 512], F32)
            incl_ps = ps2("incl_ps", [128, 2, 512], F32)
            o_ps = [ps2(f"o_ps{i}", [128, 512], F32) for i in range(2)]
            pj = [ps2(f"pj{i}", [128, 512], F32) for i in range(2)]
            wbf = [[sb2(f"wbf{s_}{i}", [128, NKC, 128], BF16) for i in range(4)] for s_ in range(2)]
            kT = sb2("kT", [128, LTOT], BF16)
            vtok = sb2("vtok", [128, NT + 1, 128], BF16)
            qz = [[sb2(f"qz{i}{h}", [128, 512], BF16) for h in range(2)] for i in range(2)]
            za = [sb2(f"za{i}", [128, 512], F32) for i in range(2)]
            zt = sb2("zt", [128, 512], F32)
            e_ = [sb2(f"e{i}", [128, 2, 512], F32) for i in range(3)]
            E2s = sb2("E2s", [128, 2, 512], F32)
            E2 = [E2s, E2s]
            sp_ = [sb2(f"sp{i}", [128, 2, 512], BF16) for i in range(2)]
            Ls = [sb2(f"Ls{i}", [128, 2, 512], BF16) for i in range(2)]
            A_ = [sb2(f"A{i}", [128, 2, 512], BF16) for i in range(2)]
            cb16 = sb2("cb16", [128, 5 * 128], BF16)
            tri16_b = cb16[:, 512:640]
            ident_b = cb16[:, 0:128]
            Tinc = cb16[:, 128:256]
            ones_b = cb16[:, 256:384]
            negm_b = cb16[:, 384:512]
            P.op("dve", lambda e: e.tensor_copy(out=cb16[:], in_=cstf[:, 0:640]), reads=["cstf"], writes=["cb16"])
            P.op("pool", lambda e: e.memset(vtok[:, 0, :], 0.0), writes=["vtm"])
            for i in range(2):
                P.op("pool", lambda e, i=i: e.memset(sp_[i][:], 0.0), writes=[f"sp{i}"])
                P.op("pool", lambda e, i=i: e.memset(A_[i][:], 0.0), writes=[f"A{i}"])
            for i in range(2):
                for h in range(2):
                    P.op("pool", lambda e, i=i, h=h: e.memset(qz[i][h][:], 0.0), writes=[f"qT{i}"])
            BK = [o_ps[0], o_ps[1], pj[0], pj[1]]
            bkctr = [0]

            def proj_fm(dst_ps, wt, cols, ncols, wkey, pkey):
                for kc in range(NKC):
                    P.op("pe", lambda e, kc=kc: e.matmul(
                        dst_ps[:, 0:ncols], lhsT=wt[:, kc, :], rhs=hT[:, kc, cols:cols + ncols],
                        start=(kc == 0), stop=(kc == NKC - 1)),
                        reads=[wkey, "hT"], writes=[pkey])

            npairs = NKC if stage >= 2 else 0
            if stage == 2:
                npairs = 1
            for hp in range(npairs):
                ws = hp % 2
                for j in range(4):
                    u = j * 8 + hp
                    P.op("pool", lambda e, j=j, u=u, ws=ws: e.dma_start(out=wbf[ws][j][:], in_=w_in_u[u, :, :, :]),
                         writes=[f"wbf{ws}{j}"], dma=f"dwbf{ws}{j}")
                wq, wk, wv, wz = wbf[ws]
                kq, kk, kv, kz = [f"wbf{ws}{j}" for j in range(4)]
                def kv_tasks(g, bK, bV):
                    kkey = "kTm" if g < 0 else f"kT{g}"
                    vkey = "vtm" if g < 0 else f"vt{g}"
                    ncols = NMETA if g < 0 else 512
                    cols = 0 if g < 0 else NMETA + g * 512
                    tasks = []
                    tasks.append(lambda: proj_fm(BK[bK], wk, cols, ncols, kk, f"bk{bK}"))
                    tasks.append(lambda: P.op("dve", lambda e: e.tensor_copy(
                        out=kT[:, cols:cols + ncols], in_=BK[bK][:, 0:ncols]),
                        reads=[f"bk{bK}"], writes=[kkey]))
                    tiles = [0] if g < 0 else list(range(4 * g + 1, 4 * g + 5))
                    rows = NMETA if g < 0 else 128

                    def vtile(si, t):
                        c0 = 0 if t == 0 else NMETA + (t - 1) * 128
                        for kc in range(NKC):
                            P.op("pe", lambda e, kc=kc, wv=wv: e.matmul(
                                BK[bV][0:rows, si * 128:(si + 1) * 128], lhsT=hT[:, kc, c0:c0 + rows],
                                rhs=wv[:, kc, :], start=(kc == 0), stop=(kc == NKC - 1)),
                                reads=[kv, "hT"], writes=[f"bk{bV}"])
                    for si, t in enumerate(tiles):
                        tasks.append(lambda si=si, t=t: vtile(si, t))
                    nt_ = len(tiles)
                    tasks.append(lambda: P.op("dve", lambda e: e.tensor_copy(
                        out=vtok[0:rows, tiles[0]:tiles[0] + nt_, :],
                        in_=BK[bV][0:rows, 0:nt_ * 128].rearrange("p (a b) -> p a b", b=128)),
                        reads=[f"bk{bV}"], writes=[vkey]))
                    return tasks

                def emit_kv(g, bK, bV):
                    for tk in kv_tasks(g, bK, bV):
                        tk()

                emit_kv(-1, 2, 3)
                emit_kv(0, 2, 3)

                def emit_group_proj_pe(g, part=None):
                    cols = NMETA + g * 512
                    p0 = 2 * (g % 2)
                    if part in (None, 0):
                        proj_fm(BK[p0], wq, cols, 512, kq, f"bk{p0}")
                    if part in (None, 1):
                        proj_fm(BK[p0 + 1], wz, cols, 512, kz, f"bk{p0 + 1}")

                def group_tasks(g1):
                    b0 = 2 * (g1 % 2)
                    return ([lambda: emit_group_proj_pe(g1, 0), lambda: emit_group_proj_pe(g1, 1), None,
                             lambda: emit_group_proj_post(g1)] + kv_tasks(g1, b0, b0 + 1))

                def emit_group_proj_post(g):
                    gb = g % 2
                    p0 = 2 * (g % 2)
                    bq, bz = BK[p0], BK[p0 + 1]
                    kbq, kbz = f"bk{p0}", f"bk{p0 + 1}"
                    for h in range(2):
                        P.op("dve", lambda e, h=h: e.tensor_scalar(
                            out=qz[gb][h][64 * h:64 * h + 64, :], in0=bq[64 * h:64 * h + 64, :], scalar1=0.125,
                            scalar2=None, op0=ALU.mult), reads=[kbq], writes=[f"qT{gb}"])
                    P.op("act", lambda e: e.activation(out=zt[:], in_=bz[:], func=AF.Exp, scale=-1.0, bias=zeroc[:]),
                         reads=[kbz, "zeroc"], writes=["zt"])
                    P.op("act", lambda e: e.activation(out=zt[:], in_=zt[:], func=AF.Ln, scale=1.0, bias=onec[:]),
                         reads=["zt", "onec"], writes=["zt"])
                    P.op("act", lambda e: e.activation(out=zt[:], in_=zt[:], func=AF.Exp, scale=-1.0, bias=zeroc[:]),
                         reads=["zt", "zeroc"], writes=["zt"])
                    P.op("dve", lambda e: e.tensor_tensor(out=za[gb][:], in0=bz[:], in1=zt[:], op=ALU.mult),
                         reads=[kbz, "zt"], writes=[f"za{gb}"])

                steps = []
                for g in range(NG):
                    kbs = list(range(4 * g + 3, -1, -1)) + [-1]
                    for si, kb in enumerate(kbs):
                        diag = kb >= 4 * g
                        c0 = (kb - 4 * g) * 128 if diag else 0
                        steps.append(dict(g=g, kb=kb, P=(NMETA if kb < 0 else 128), c0=c0, diag=diag,
                                          first=(si == 0), last=(si == len(kbs) - 1), si=si, ns=len(kbs)))
                n = len(steps)

                def QK(i):
                    s_ = steps[i]
                    Pn, c0, g = s_["P"], s_["c0"], s_["g"]
                    kc0 = 0 if s_["kb"] < 0 else NMETA + s_["kb"] * 128
                    dg = s_["diag"]
                    kkey_ = "kTm" if s_["kb"] < 0 else f"kT{s_['kb'] // 4}"
                    for h in range(2):
                        P.op("pe", lambda e, h=h: e.matmul(
                            z_ps[0:Pn, h, c0:512], lhsT=kT[:, kc0:kc0 + Pn],
                            rhs=qz[g % 2][h][:, c0:512], start=True, stop=(not dg)),
                            reads=[kkey_, f"qT{g % 2}"], writes=["z_ps"])
                        if dg:
                            P.op("pe", lambda e, h=h: e.matmul(
                                z_ps[0:128, h, c0:c0 + 128], lhsT=ident_b, rhs=negm_b, start=False, stop=True),
                                reads=["cb16"], writes=["z_ps"])

                emit_group_proj_pe(0)
                emit_group_proj_post(0)
                QK(0)
                for i in range(0, n + 1):
                    b = i % 2
                    pb_ = (i - 1) % 2
                    eb = i % 3
                    epb = (i - 1) % 3
                    s_ = steps[i] if i < n else None
                    sp_prev = steps[i - 1] if i >= 1 else None
                    if s_ is not None:
                        Pn, c0 = s_["P"], s_["c0"]
                        P.op("act", lambda e, eb=eb, Pn=Pn, c0=c0: e.activation(
                            out=e_[eb][0:Pn, :, c0:512], in_=z_ps[0:Pn, :, c0:512], func=AF.Exp,
                            scale=1.0, bias=zeroc[0:Pn, :]), reads=["z_ps", "zeroc"], writes=[f"e{eb}"])
                        if s_["g"] + 1 < NG:
                            if s_["si"] == 0:
                                gq = group_tasks(s_["g"] + 1)
                            elif s_["last"]:
                                while gq:
                                    tk = gq.pop(0)
                                    if tk is not None:
                                        tk()
                            elif gq:
                                tk = gq.pop(0)
                                if tk is not None:
                                    tk()
                    if i + 1 < n:
                        QK(i + 1)
                    if s_ is not None:
                        P.op("act", lambda e, b=b, eb=eb, Pn=Pn, c0=c0: e.activation(
                            out=sp_[b][0:Pn, :, c0:512], in_=e_[eb][0:Pn, :, c0:512], func=AF.Ln,
                            scale=1.0, bias=onec[0:Pn, :]), reads=[f"e{eb}", "onec"], writes=[f"sp{b}"])
                    if sp_prev is not None:
                        Pp, cp = sp_prev["P"], sp_prev["c0"]
                        P.op("act", lambda e, pb_=pb_, Pp=Pp, cp=cp: e.activation(
                            out=E2[pb_][0:Pp, :, cp:512], in_=incl_ps[0:Pp, :, cp:512], func=AF.Exp,
                            scale=-1.0, bias=zeroc[0:Pp, :]), reads=["incl_ps", "zeroc"], writes=["E2s"])
                    if s_ is not None:
                        Pn, c0 = s_["P"], s_["c0"]
                        c0o = c0 + 128 if s_["diag"] else 0
                        lc, ln_ = f"Ls{b}", f"Ls{1 - b}"
                        has_old = c0o < 512
                        for h in range(2):
                            lt = Tinc[:, :] if Pn == 128 else tri16_b[:, 0:Pn]
                            P.op("pe", lambda e, b=b, Pn=Pn, c0=c0, h=h, has_old=has_old, lt=lt: e.matmul(
                                incl_ps[0:Pn, h, c0:512], lhsT=lt, rhs=sp_[b][:, h, c0:512],
                                start=True, stop=(not has_old)),
                                reads=[f"sp{b}", "cb16"], writes=["incl_ps"])
                            if has_old:
                                P.op("pe", lambda e, b=b, Pn=Pn, c0o=c0o, h=h: e.matmul(
                                    incl_ps[0:Pn, h, c0o:512], lhsT=ones_b[:, 0:Pn], rhs=Ls[b][:, h, c0o:512],
                                    start=False, stop=True),
                                    reads=[lc, "cb16"], writes=["incl_ps"])
                        if not s_["last"]:
                            if has_old:
                                P.op("dve", lambda e, b=b, c0o=c0o: e.tensor_tensor(
                                    out=Ls[1 - b][:, :, c0o:512], in0=Ls[b][:, :, c0o:512],
                                    in1=sp_[b][:, :, c0o:512], op=ALU.add),
                                    reads=[lc, f"sp{b}"], writes=[ln_])
                            if s_["diag"]:
                                P.op("dve", lambda e, b=b, c0=c0: e.tensor_copy(
                                    out=Ls[1 - b][:, :, c0:c0 + 128], in_=sp_[b][:, :, c0:c0 + 128]),
                                    reads=[f"sp{b}"], writes=[ln_])
                    if sp_prev is not None:
                        Pp, cp = sp_prev["P"], sp_prev["c0"]
                        P.op("dve", lambda e, pb_=pb_, epb=epb, Pp=Pp, cp=cp: e.tensor_tensor(
                            out=A_[pb_][0:Pp, :, cp:512], in0=e_[epb][0:Pp, :, cp:512],
                            in1=E2[pb_][0:Pp, :, cp:512], op=ALU.mult),
                            reads=[f"e{epb}", "E2s"], writes=[f"A{pb_}"])
                        vtile = 0 if sp_prev["kb"] < 0 else sp_prev["kb"] + 1
                        vkey_ = "vtm" if sp_prev["kb"] < 0 else f"vt{sp_prev['kb'] // 4}"
                        for h in range(2):
                            ob = 2 * (sp_prev["g"] % 2) + h
                            P.op("pe", lambda e, pb_=pb_, Pp=Pp, cp=cp, h=h, vtile=vtile, f_=sp_prev["first"],
                                 l_=sp_prev["last"], ob=ob: e.matmul(
                                BK[ob][:, cp:512], lhsT=vtok[:, vtile, :], rhs=A_[pb_][:, h, cp:512],
                                start=f_, stop=l_, skip_group_check=True),
                                reads=[f"A{pb_}", vkey_], writes=[f"bk{ob}"])
                        if sp_prev["last"]:
                            g = sp_prev["g"]
                            for h in range(2):
                                ob = 2 * (g % 2) + h
                                P.op("dve", lambda e, h=h, g=g, hp=hp, ob=ob: e.tensor_tensor(
                                    out=ogT[64 * h:64 * h + 64, hp, g * 512:(g + 1) * 512],
                                    in0=BK[ob][64 * h:64 * h + 64, :], in1=za[g % 2][64 * h:64 * h + 64, :],
                                    op=ALU.mult), reads=[f"bk{ob}", f"za{g % 2}"], writes=[f"ogT{g}"])
            P.emit()
        if dbg:
            P.op("sp", lambda e: e.dma_start(out=dbg_hT[:, :, :], in_=hT[:]),
                 reads=["hT"], writes=["dbg_hT"], dma="o0")
            P.final_waits("sp", ["o0"])
            P.emit()
        s12.close()


        nblk = NG if stage >= 3 else 0
        with contextlib.ExitStack() as s3:
            sb3 = lambda name, shape, dt: s3.enter_context(nc.sbuf_tensor(name, shape, dt))
            ps3 = lambda name, shape, dt: s3.enter_context(nc.psum_tensor(name, shape, dt))
            tp3 = ps3("tp3", [128, NKC, 128], F32)
            pp = [ps3(f"pp{i}", [128, 512], F32) for i in range(4)]
            stp = [ps3(f"stp{i}", [128, 512], F32) for i in range(2)]
            xt = [sb3(f"x3t{i}", [128, D], F32) for i in range(2)]
            st = sb3("st3", [128, 2, 6], F32)
            st1 = sb3("st3b", [128, 2, 6], F32)
            xs = [sb3(f"x3s{i}", [128, D], F32) for i in range(2)]
            mv = sb3("mv3", [128, 2], F32)
            mvs = sb3("mvs", [128, 4, 2], F32)
            rss = sb3("rss", [128, 4], F32)
            rs5 = sb3("rs5", [128, 1], F32)
            hTb = sb3("hTb", [128, NKC, NMETA + 512], BF16)
            ub = sb3("ub", [128, NKC, 30 + 512], BF16)
            diag = sb3("diag", [128, CONVK, 128], BF16)
            c_sb = sb3("c_sb", [128, NKC, 512], F32)
            T = [sb3(f"T{i}", [128, 512], F32) for i in range(7)]
            ycT = sb3("ycT", [128, NKC, 512], BF16)
            hmT = sb3("hmT", [128, NKC, 512], BF16)
            ro = [sb3(f"ro{i}", [128, D], F32) for i in range(2)]
            NWB = 10
            wb3 = [sb3(f"w3b{i}", [128, NKC, 128], BF16) for i in range(NWB)]
            wo0 = sb3("wo0", [128, NKC, 512], BF16)
            gb = [sb3(f"gb{i}", [128, D], F32) for i in range(4)]
            ncgb = sb3("ncgb", [128, 2 * NKC], F32)
            wo1 = ub[:, :, 30:542]
            onesf = cstf[:, 256:384]
            dwb = lambda cc: pv[:, 2 * NKC + cc:2 * NKC + cc + 1]
            cg = lambda cc: pv[:, 3 * NKC + cc:3 * NKC + cc + 1]
            cbi = lambda cc: pv[:, 4 * NKC + cc:4 * NKC + cc + 1]
            ncg = lambda cc: ncgb[:, cc:cc + 1]
            ncb = lambda cc: ncgb[:, NKC + cc:NKC + cc + 1]

            if nblk:
                for i in range(4):
                    P.op("sp", lambda e, i=i: e.dma_start(out=gb[i][:], in_=gbb[i, :, :]), writes=[f"gb{i}"], dma=f"gb{i}")
                for i in range(2):
                    P.op("dve", lambda e, i=i: e.tensor_scalar(out=gb[i][:], in0=gb[i][:], scalar1=ALPHA, scalar2=None,
                                                               op0=ALU.mult), reads=[f"gb{i}"], writes=[f"gb{i}"])
                P.op("dve", lambda e: e.tensor_scalar(out=ncgb[:], in0=pv[:, 3 * NKC:5 * NKC], scalar1=-1.0, scalar2=None,
                                                      op0=ALU.mult), reads=["pv"], writes=["ncgb"])
                P.op("pool", lambda e: e.memset(ub[:, :, 0:30], 0.0), writes=[f"ub{cc}" for cc in range(NKC)])

            wctr = [0]
            pend = []
            LAG = 4

            def wload(src_ap, dst_ap=None, dst_key=None):
                if dst_ap is None:
                    bi = wctr[0] % NWB
                    wctr[0] += 1
                    dst_ap, dst_key = wb3[bi][:], f"w3b{bi}"
                    ret = (wb3[bi], dst_key)
                    dk = dst_key
                else:
                    ret = (None, dst_key)
                    dk = dst_key if isinstance(dst_key, str) else dst_key[0]
                dkeys = dst_key if isinstance(dst_key, list) else [dst_key]
                P.op("pool", lambda e: e.dma_start(out=dst_ap, in_=src_ap), writes=dkeys, dma="d" + dk)
                return ret

            def flush():
                while pend:
                    pend.pop(0)()

            def item(specs, fn):
                tiles = [wload(*sp) for sp in specs]
                pend.append(lambda: fn(*tiles))
                while len(pend) > LAG:
                    pend.pop(0)()

            def proj3(dst_ps, pkey, wt, wkey, rhs_fn, rkeys, ncols=512, c0=0):
                for kc in range(NKC):
                    P.op("pe", lambda e, kc=kc, rhs=rhs_fn(kc): e.matmul(dst_ps[:, c0:c0 + ncols], lhsT=wt[:, kc, :], rhs=rhs,
                                                         start=(kc == 0), stop=(kc == NKC - 1)),
                         reads=[wkey] + rkeys, writes=[pkey])

            def sigm(dst, dkey, src, skey, ncols=512, c0=0, scale=None, bias=None, extra=()):
                d = dst[:, c0:c0 + ncols]
                P.op("act", lambda e: e.activation(out=d, in_=src, func=AF.Exp,
                                                   scale=(-1.0 if scale is None else scale),
                                                   bias=(zeroc[:] if bias is None else bias)),
                     reads=[skey, "zeroc"] + list(extra), writes=[dkey])
                P.op("act", lambda e: e.activation(out=d, in_=d, func=AF.Ln, scale=1.0, bias=onec[:]),
                     reads=[dkey, "onec"], writes=[dkey])
                P.op("act", lambda e: e.activation(out=d, in_=d, func=AF.Exp, scale=-1.0, bias=zeroc[:]),
                     reads=[dkey, "zeroc"], writes=[dkey])

            def ln_rstd(dst, dkey, var_ap, vkey, rows=128):
                P.op("act", lambda e: e.activation(out=dst, in_=var_ap, func=AF.Ln, scale=1.0, bias=epsc[0:rows, :]),
                     reads=[vkey, "epsc"], writes=[dkey])
                P.op("act", lambda e: e.activation(out=dst, in_=dst, func=AF.Exp, scale=-0.5, bias=zeroc[0:rows, :]),
                     reads=[dkey, "zeroc"], writes=[dkey])

            hblk = lambda kc: hTb[:, kc, NMETA:NMETA + 512]
            hmeta = lambda kc: hTb[:, kc, 0:NMETA]

            def s1_tile(blk, t, xbuf, xk, dq):
                rows = NMETA if t < 0 else 128
                src = meta[:, :] if t < 0 else x[blk * 512 + t * 128:blk * 512 + (t + 1) * 128, :]
                c0 = 0 if t < 0 else NMETA + t * 128
                P.op(dq, lambda e: e.dma_start(out=xbuf[0:rows, :], in_=src), writes=[xk], dma=xk)
                for h in range(2):
                    P.op("dve", lambda e, h=h: e.bn_stats(
                        out=st1[0:rows, h, :], in_=xbuf[0:rows, h * 512:(h + 1) * 512]),
                        reads=[xk], writes=["st1"])
                if t < 0:
                    mva, mkey, rsa, rkey = mv[0:rows, :], "mv3", rs5[0:rows, :], "rs5"
                else:
                    mva, mkey, rsa, rkey = mvs[:, t, :], f"mvs{t}", rss[:, t:t + 1], f"rss{t}"
                P.op("dve", lambda e: e.bn_aggr(out=mva, in_=st1[0:rows, :, :]), reads=["st1"], writes=[mkey])
                ln_rstd(rsa, rkey, mva[:, 1:2], mkey, rows)
                P.op("dve", lambda e: e.tensor_scalar(
                    out=xbuf[0:rows, :], in0=xbuf[0:rows, :], scalar1=mva[:, 0:1], scalar2=rsa,
                    op0=ALU.subtract, op1=ALU.mult), reads=[xk, mkey, rkey], writes=[xk])
                for kc in range(NKC):
                    P.op("pe", lambda e, kc=kc: e.transpose(
                        out=tp3[:, kc, 0:rows], in_=xbuf[0:rows, kc * 128:(kc + 1) * 128],
                        identity=identf[0:rows, 0:rows]), reads=[xk, "cstf"], writes=["tp3"])
                for kc in range(NKC):
                    P.op("act", lambda e, kc=kc: e.activation(
                        out=hTb[:, kc, c0:c0 + rows], in_=tp3[:, kc, 0:rows], func=AF.Identity,
                        scale=embg(kc), bias=embb(kc)), reads=["tp3", "pv"], writes=["hTb"])

            def s1_body(blk):
                tiles = ([-1] if blk == 0 else []) + list(range(4))
                for t in tiles:
                    xb = (t + 1) % 2
                    s1_tile(blk, t, xt[xb], f"x3t{xb}", "sp")

            def s2a_body(blk, cc, wuk, wgk):
                (wu, ku), (wg, kg) = wuk, wgk
                proj3(pp[0], "pp0", wu, ku, hblk, ["hTb"])
                proj3(pp[1], "pp1", wg, kg, hblk, ["hTb"])
                if blk == 0:
                    proj3(pp[2], "pp2", wu, ku, hmeta, ["hTb"], ncols=NMETA, c0=0)
                    proj3(pp[2], "pp2", wg, kg, hmeta, ["hTb"], ncols=NMETA, c0=NMETA)
                    sigm(T[1], "T1", pp[2][:, NMETA:2 * NMETA], "pp2", ncols=NMETA)
                    P.op("dve", lambda e: e.tensor_tensor(out=ub[:, cc, 14:30], in0=pp[2][:, 0:NMETA],
                                                          in1=T[1][:, 0:NMETA], op=ALU.mult),
                         reads=["pp2", "T1"], writes=[f"ub{cc}"])
                sigm(T[0], "T0", pp[1][:], "pp1")
                P.op("dve", lambda e: e.tensor_tensor(out=ub[:, cc, 30:542], in0=pp[0][:], in1=T[0][:],
                                                      op=ALU.mult), reads=["pp0", "T0"], writes=[f"ub{cc}"])

            def s2b_body(cc):
                dwo = 5 * NKC + cc * CONVK
                P.op("pool", lambda e: e.tensor_tensor(
                    out=diag[:], in0=cstf[:, 0:128].unsqueeze(1).broadcast_to([128, CONVK, 128]),
                    in1=pv[:, dwo:dwo + CONVK].unsqueeze(2).broadcast_to([128, CONVK, 128]), op=ALU.mult),
                    reads=["cstf", "pv"], writes=["diag"])
                for k in range(CONVK):
                    P.op("pe", lambda e, k=k: e.matmul(pp[3][:], lhsT=diag[:, k, :], rhs=ub[:, cc, k:k + 512],
                                                       start=(k == 0), stop=(k == CONVK - 1)),
                         reads=["diag", f"ub{cc}"], writes=["pp3"])
                P.op("act", lambda e: e.activation(out=c_sb[:, cc, :], in_=pp[3][:], func=AF.Identity,
                                                   scale=1.0, bias=dwb(cc)),
                     reads=["pp3", "pv"], writes=[f"c_sb{cc}"])
                tq = T[2 + cc % 2]
                tqk = f"T{2 + cc % 2}"
                P.op("act", lambda e: e.activation(out=tq[:], in_=c_sb[:, cc, :], func=AF.Square,
                                                   scale=1.0, bias=zeroc[:]),
                     reads=[f"c_sb{cc}", "zeroc"], writes=[tqk])

            def s2c_body(cc):
                tq = T[2 + cc % 2]
                tqk = f"T{2 + cc % 2}"
                P.op("pe", lambda e: e.matmul(stp[0][:], lhsT=onesf, rhs=c_sb[:, cc, :],
                                              start=(cc == 0), stop=(cc == NKC - 1)),
                     reads=["cstf", f"c_sb{cc}"], writes=["stp0"])
                P.op("pe", lambda e: e.matmul(stp[1][:], lhsT=onesf, rhs=tq[:],
                                              start=(cc == 0), stop=(cc == NKC - 1)),
                     reads=["cstf", tqk], writes=["stp1"])

            def s3pre_body():
                P.op("pool", lambda e: e.tensor_copy(out=ub[:, :, 0:30], in_=ub[:, :, 512:542]),
                     reads=[f"ub{cc}" for cc in range(NKC)], writes=[f"ub{cc}" for cc in range(NKC)])
                P.op("dve", lambda e: e.tensor_scalar(out=T[2][:], in0=stp[0][:], scalar1=1.0 / D, scalar2=None,
                                                      op0=ALU.mult), reads=["stp0"], writes=["T2"])
                P.op("dve", lambda e: e.tensor_tensor(out=T[4][:], in0=T[2][:], in1=T[2][:], op=ALU.mult),
                     reads=["T2"], writes=["T4"])
                P.op("dve", lambda e: e.scalar_tensor_tensor(out=T[3][:], in0=stp[1][:], scalar=1.0 / D, in1=T[4][:],
                                                             op0=ALU.mult, op1=ALU.subtract),
                     reads=["stp1", "T4"], writes=["T3"])
                ln_rstd(T[3][:], "T3", T[3][:], "T3")

            def s3_body(cc, wzk):
                wz, kz = wzk
                proj3(pp[0], "pp0", wz, kz, hblk, ["hTb"])
                P.op("dve", lambda e: e.tensor_tensor(out=T[0][:], in0=c_sb[:, cc, :], in1=T[2][:],
                                                      op=ALU.subtract),
                     reads=[f"c_sb{cc}", "T2"], writes=["T0"])
                P.op("dve", lambda e: e.tensor_tensor(out=T[0][:], in0=T[0][:], in1=T[3][:], op=ALU.mult),
                     reads=["T0", "T3"], writes=["T0"])
                P.op("dve", lambda e: e.tensor_scalar(out=T[1][:], in0=T[0][:], scalar1=cg(cc), scalar2=cbi(cc),
                                                      op0=ALU.mult, op1=ALU.add),
                     reads=["T0", "pv"], writes=["T1"])
                sigm(T[5], "T5", T[0][:], "T0", scale=ncg(cc), bias=ncb(cc), extra=["ncgb"])
                sigm(T[4], "T4", pp[0][:], "pp0")
                P.op("dve", lambda e: e.tensor_tensor(out=T[1][:], in0=T[1][:], in1=T[5][:], op=ALU.mult),
                     reads=["T1", "T5"], writes=["T1"])
                P.op("dve", lambda e: e.tensor_tensor(out=T[1][:], in0=T[1][:], in1=pp[0][:], op=ALU.mult),
                     reads=["T1", "pp0"], writes=["T1"])
                P.op("dve", lambda e: e.tensor_tensor(out=ycT[:, cc, :], in0=T[1][:], in1=T[4][:],
                                                      op=ALU.mult),
                     reads=["T1", "T4"], writes=["ycT"])

            def s4b_body(blk, m, waok, wgak):
                (wao, kao), (wga, kga) = waok, wgak
                proj3(pp[2], "pp2", wao, kao, lambda kc: ogT[:, kc, blk * 512:(blk + 1) * 512], [f"ogT{blk}"])
                proj3(pp[3], "pp3", wga, kga, hblk, ["hTb"])
                sigm(T[6], "T6", pp[3][:], "pp3")
                P.op("dve", lambda e: e.tensor_tensor(out=c_sb[:, m, :], in0=pp[2][:], in1=T[6][:], op=ALU.mult),
                     reads=["pp2", "T6"], writes=[f"c_sb{m}"])

            def s4a_body(m, wcok, wgck):
                (wco, kco), (wgc, kgc) = wcok, wgck
                q = m % 2
                pa, pb, ka, kb_ = pp[2 * q], pp[2 * q + 1], f"pp{2 * q}", f"pp{2 * q + 1}"
                ta, tb, kta, ktb = T[q], T[4 + q], f"T{q}", f"T{4 + q}"
                proj3(pa, ka, wco, kco, lambda kc: ycT[:, kc, :], ["ycT"])
                proj3(pb, kb_, wgc, kgc, hblk, ["hTb"])
                sigm(ta, kta, pb[:], kb_)
                P.op("dve", lambda e: e.tensor_tensor(out=tb[:], in0=pa[:], in1=ta[:], op=ALU.mult),
                     reads=[ka, kta], writes=[ktb])
                P.op("dve", lambda e: e.tensor_tensor(out=hmT[:, m, :], in0=tb[:], in1=c_sb[:, m, :], op=ALU.add),
                     reads=[ktb, f"c_sb{m}"], writes=["hmT"])

            wo1keys = [f"ub{cc}" for cc in range(NKC)] + [f"wo1_{c}" for c in range(4)]
            wo0keys = [f"wo0_{c}" for c in range(4)]

            def s5_body(blk, *_):
                def partA(t):
                    xb = t % 2
                    xk = f"x3t{xb}"
                    src = x[blk * 512 + t * 128:blk * 512 + (t + 1) * 128, :]
                    P.op("act", lambda e: e.dma_start(out=xt[xb][:], in_=src), writes=[xk], dma=xk)
                    P.op("dve", lambda e: e.tensor_scalar(
                        out=xt[xb][:], in0=xt[xb][:], scalar1=mvs[:, t, 0:1], scalar2=rss[:, t:t + 1],
                        op0=ALU.subtract, op1=ALU.mult), reads=[xk, f"mvs{t}", f"rss{t}"], writes=[xk])
                    P.op("dve", lambda e: e.tensor_tensor(out=xt[xb][:], in0=xt[xb][:], in1=gb[0][:], op=ALU.mult),
                         reads=[xk, "gb0"], writes=[xk])
                    P.op("dve", lambda e: e.tensor_tensor(out=xt[xb][:], in0=xt[xb][:], in1=gb[1][:], op=ALU.add),
                         reads=[xk, "gb1"], writes=[xk])

                partA(0)
                for t in range(4):
                    xb = t % 2
                    rb = t % 2
                    xk = f"x3t{xb}"
                    pa, pb = (pp[0], pp[1]) if t % 2 == 0 else (pp[2], pp[3])
                    pak, pbk = ("pp0", "pp1") if t % 2 == 0 else ("pp2", "pp3")
                    for m in range(NKC):
                        P.op("pe", lambda e, m=m, t=t, pa=pa: e.matmul(
                            pa[:], lhsT=hmT[:, m, t * 128:(t + 1) * 128], rhs=wo0[:, m, :],
                            start=(m == 0), stop=(m == NKC - 1)), reads=["hmT"] + wo0keys, writes=[pak])
                    for m in range(NKC):
                        P.op("pe", lambda e, m=m, t=t, pb=pb: e.matmul(
                            pb[:], lhsT=hmT[:, m, t * 128:(t + 1) * 128], rhs=wo1[:, m, :],
                            start=(m == 0), stop=(m == NKC - 1)), reads=["hmT"] + wo1keys, writes=[pbk])
                    if t + 1 < 4:
                        partA(t + 1)
                    P.op("dve", lambda e, rb=rb, pa=pa, xb=xb: e.tensor_tensor(out=ro[rb][:, 0:512], in0=xt[xb][:, 0:512],
                                                                       in1=pa[:], op=ALU.add),
                         reads=[xk, pak], writes=[f"ro{rb}"])
                    P.op("dve", lambda e, rb=rb, pb=pb, xb=xb: e.tensor_tensor(out=ro[rb][:, 512:1024], in0=xt[xb][:, 512:1024],
                                                                       in1=pb[:], op=ALU.add),
                         reads=[xk, pbk], writes=[f"ro{rb}"])
                    for h in range(2):
                        P.op("dve", lambda e, rb=rb, h=h: e.bn_stats(out=st[:, h, :], in_=ro[rb][:, h * 512:(h + 1) * 512]),
                             reads=[f"ro{rb}"], writes=["st3"])
                    P.op("dve", lambda e: e.bn_aggr(out=mv[:], in_=st[:, :, :]), reads=["st3"], writes=["mv3"])
                    ln_rstd(rs5[:], "rs5", mv[:, 1:2], "mv3")
                    P.op("dve", lambda e, rb=rb: e.tensor_scalar(
                        out=ro[rb][:], in0=ro[rb][:], scalar1=mv[:, 0:1], scalar2=rs5[:],
                        op0=ALU.subtract, op1=ALU.mult), reads=[f"ro{rb}", "mv3", "rs5"], writes=[f"ro{rb}"])
                    P.op("dve", lambda e, rb=rb: e.tensor_tensor(out=ro[rb][:], in0=ro[rb][:], in1=gb[2][:], op=ALU.mult),
                         reads=[f"ro{rb}", "gb2"], writes=[f"ro{rb}"])
                    P.op("dve", lambda e, rb=rb: e.tensor_tensor(out=ro[rb][:], in0=ro[rb][:], in1=gb[3][:], op=ALU.add),
                         reads=[f"ro{rb}", "gb3"], writes=[f"ro{rb}"])
                    P.op("sp", lambda e, rb=rb, t=t: e.dma_start(
                        out=out[blk * 512 + t * 128:blk * 512 + (t + 1) * 128, :], in_=ro[rb][:]),
                        reads=[f"ro{rb}"], writes=["out_hbm"], dma=f"oro{rb}")
                    if blk + 1 < nblk:
                        s1_tile(blk + 1, t, xs[t % 2], f"x3s{t % 2}", "act")

            for blk in range(nblk):
                if blk == 0:
                    item([], lambda blk=blk: s1_body(blk))
                wsp = lambda cc: [(w_in_u[32 + cc, :, :, :],), (w_in_u[40 + cc, :, :, :],)]
                for cc in range(NKC + 2):
                    if cc < NKC:
                        item(wsp(cc), lambda a, b_, blk=blk, cc=cc: s2a_body(blk, cc, a, b_))
                    if 0 <= cc - 1 < NKC:
                        item([], lambda cc=cc: s2b_body(cc - 1))
                    if 0 <= cc - 2 < NKC:
                        item([], lambda cc=cc: s2c_body(cc - 2))
                item([], s3pre_body)
                for cc in range(NKC + 1):
                    if cc < NKC:
                        item([(w_in_u[48 + cc, :, :, :],)], lambda a, cc=cc: s3_body(cc, a))
                    if cc >= 1:
                        m = cc - 1
                        item([(w_ao_u[m, :, :, :],), (w_in_u[56 + m, :, :, :],)],
                             lambda a, b_, blk=blk, m=m: s4b_body(blk, m, a, b_))
                for m in range(NKC):
                    item([(w_co_u[m, :, :, :],), (w_in_u[64 + m, :, :, :],)], lambda a, b_, m=m: s4a_body(m, a, b_))
                specs = []
                for c in range(4):
                    specs.append((w_o_u[0, :, :, c * 128:(c + 1) * 128], wo0[:, :, c * 128:(c + 1) * 128], f"wo0_{c}"))
                for c in range(4):
                    specs.append((w_o_u[1, :, :, c * 128:(c + 1) * 128], wo1[:, :, c * 128:(c + 1) * 128],
                                  [f"wo1_{c}"] + [f"ub{cc}" for cc in range(NKC)]))
                item(specs, lambda *a, blk=blk: s5_body(blk, *a))
            flush()
            P.final_waits("sp", [k for k in P.cnt if k.startswith("o")])
            P.emit()

        if dbg:
            P.op("sp", lambda e: e.dma_start(out=dbg_og[:, :, :], in_=ogT[:]),
                 reads=[f"ogT{g}" for g in range(NG)], writes=["dbg_og"], dma="o0")
        P.final_waits("sp", [k for k in P.cnt if k.startswith("o")])
        P.emit()
    return nc


def _host_layout(inp, b):
    f = lambda a: np.ascontiguousarray(a, dtype=np.float32)
    pm = lambda v: f(np.asarray(v).reshape(NKC, 128).T)
    dw = np.asarray(inp["dw_w"][0])
    dwp = dw.T.reshape(NKC, 128, CONVK).transpose(1, 0, 2).reshape(128, NKC * CONVK)
    pvec = np.concatenate([pm(inp["emb_ln_g"]), pm(inp["emb_ln_b"]), pm(inp["dw_b"][0]),
                           pm(inp["conv_ln_g"][0]), pm(inp["conv_ln_b"][0]), f(dwp)], axis=1)
    gbb = np.stack([np.broadcast_to(np.asarray(v).reshape(1, D), (128, D)) for v in
                    (inp["emb_ln_g"], inp["emb_ln_b"], inp["post_ln_g"][0], inp["post_ln_b"][0])])

    def units(w, ncols):
        w = np.asarray(w)
        n = w.shape[1] // ncols
        return f(w.reshape(NKC, 128, n, ncols).transpose(2, 1, 0, 3))
    ii = np.arange(128)
    ident = (ii[:, None] == ii[None, :])
    tri = (ii[:, None] >= ii[None, :])
    ones = np.ones((128, 128), bool)
    negm = np.where(ii[:, None] >= ii[None, :], -30000.0, 0.0)
    tri16 = tri & (ii[:, None] < NMETA)
    cst = np.concatenate([ident, tri, ones, negm, tri16], axis=1).astype(np.float32)
    return {
        "x": f(inp["x"][b]), "meta": f(inp["meta_tokens"]), "pvec": f(pvec), "gbb": f(gbb),
        "w_in_u": units(inp["w_in"][0], 128), "w_ao_u": units(inp["w_attn_out"][0], 128),
        "w_co_u": units(inp["w_conv_out"][0], 128), "w_o_u": units(inp["w_out"][0], 512),
        "cst": f(cst),
    }


def kernel(**inputs):
    nc = build()
    shared = _host_layout(inputs, 0)
    in_maps = []
    for b in range(8):
        m = dict(shared)
        m["x"] = np.ascontiguousarray(inputs["x"][b], dtype=np.float32)
        in_maps.append(m)
    res = run_bass_kernel_spmd(nc, in_maps, core_ids=list(range(8)))
    return np.stack([np.asarray(r["out"]) for r in res.results], axis=0).astype(np.float32)
```
